# Optimizing a Trainium2 kernel written in Bass

```python
import jax, jax.numpy as jnp
from jax import lax
import numpy as np

D_MODEL = 2048
BATCH = 4
SEQ = 2048
DEPTH = 4
DEC_BATCH = 128
DEC_SEQ = 4
PAST_LEN = 16384
PAGE_SIZE = 128

POOL_WINDOWS = (2, 4, 8, 16)
N_POOL_GROUPS = len(POOL_WINDOWS)
POOL_GROUP = D_MODEL // 16
POOL_WIDTH = N_POOL_GROUPS * POOL_GROUP
POOL_BUF = max(POOL_WINDOWS) - 1
CONV_WIDTH = D_MODEL // 2
CONV_K = 3
CONV_BUF = CONV_K - 1
CHUNK = 128
N_SGU_GROUPS = 4
SGU_GROUP = D_MODEL // 16
SGU_WIDTH = N_SGU_GROUPS * SGU_GROUP
N_BRANCH = 3
D_FF = 4 * D_MODEL
N_MOD = 6
EPS = 1e-6
OFF_POOL = 0
OFF_CONV = OFF_POOL + POOL_WIDTH
OFF_SGU = OFF_CONV + 3 * CONV_WIDTH
OFF_GATE = OFF_SGU + 2 * SGU_WIDTH
N_IN = OFF_GATE + N_BRANCH * D_MODEL

kernel_name = "hybrid_pool_conv_sgu_decoder_step"


def rmsnorm(x, g):
    xf = x.astype(jnp.float32)
    r = xf * lax.rsqrt(jnp.mean(xf * xf, axis=-1, keepdims=True) + EPS)
    return (r * g.astype(jnp.float32)).astype(x.dtype)


def pool_mix(p, prefix, pos, w_grp, scale):
    N, T, _ = p.shape
    full = jnp.concatenate([prefix, p], axis=1).astype(jnp.float32)
    cs = jnp.pad(jnp.cumsum(full, axis=1), ((0, 0), (1, 0), (0, 0)))
    hi = cs[:, POOL_BUF + 1:POOL_BUF + 1 + T]
    pf = p.astype(jnp.float32)
    outs = []
    for g, w in enumerate(POOL_WINDOWS):
        sl = slice(g * POOL_GROUP, (g + 1) * POOL_GROUP)
        lo = cs[:, POOL_BUF + 1 - w:POOL_BUF + 1 - w + T, sl]
        cnt = jnp.minimum(pos + 1, w).astype(jnp.float32)[None, :, None]
        outs.append((hi[..., sl] - lo) / cnt - pf[..., sl])
    d = jnp.stack(outs, axis=2)
    y = jnp.einsum('ntgc,gcd->ntgd', d, w_grp).reshape(N, T, POOL_WIDTH) * scale
    return y.astype(p.dtype)


def short_conv(z, prefix, w_conv):
    T = z.shape[1]
    full = jnp.concatenate([prefix, z], axis=1)
    return sum(w_conv[k] * full[:, k:k + T] for k in range(CONV_K))


def spatial_gate(v, w_s, b_s, L):
    N, T, _ = v.shape
    vf = v.reshape(N, T // L, L, N_SGU_GROUPS, SGU_GROUP)
    w = jnp.tril(w_s[:, :L, :L])
    out = jnp.einsum('gij,nkjgc->nkigc', w, vf) + b_s[:, :L].T[None, None, :, :, None]
    return out.reshape(N, T, SGU_WIDTH)


def layer(x, c, pool_prefix, conv_prefix, pos, L, norm1, norm2, w_ada, b_ada, w_in,
          w_pool_grp, pool_scale, w_conv, sgu_norm, w_sgu, b_sgu,
          w_br_pool, w_br_conv, w_br_sgu, w_out, w_ff1, w_ff2):
    N, T, D = x.shape
    mod = (jax.nn.silu(c) @ w_ada + b_ada).reshape(N, 1, N_MOD, D)
    sh1, sc1, g1, sh2, sc2, g2 = [mod[:, :, i] for i in range(N_MOD)]
    h = rmsnorm(x, norm1) * (1 + sc1) + sh1
    proj = h @ w_in
    p = proj[..., OFF_POOL:OFF_CONV]
    xc, bc, cc = jnp.split(proj[..., OFF_CONV:OFF_SGU], 3, axis=-1)
    u, v = jnp.split(proj[..., OFF_SGU:OFF_GATE], 2, axis=-1)
    gates = jax.nn.sigmoid(proj[..., OFF_GATE:]).reshape(N, T, N_BRANCH, D)
    pool_out = pool_mix(p, pool_prefix, pos, w_pool_grp, pool_scale)
    z = cc * xc
    conv_out = bc * short_conv(z, conv_prefix, w_conv)
    v = rmsnorm(jax.nn.gelu(v), sgu_norm)
    sgu_out = jax.nn.gelu(u) * spatial_gate(v, w_sgu, b_sgu, L)
    merged = (gates[:, :, 0] * (pool_out @ w_br_pool)
              + gates[:, :, 1] * (conv_out @ w_br_conv)
              + gates[:, :, 2] * (sgu_out @ w_br_sgu))
    x = x + g1 * (merged @ w_out)
    h2 = rmsnorm(x, norm2) * (1 + sc2) + sh2
    x = x + g2 * (jnp.square(jax.nn.relu(h2 @ w_ff1)) @ w_ff2)
    new_pool = jnp.concatenate([pool_prefix, p], axis=1)[:, -POOL_BUF:]
    new_conv = jnp.concatenate([conv_prefix, z], axis=1)[:, -CONV_BUF:]
    new_v = v[:, T - L:]
    return x, new_pool, new_conv, new_v


def setup_inputs(seed: int = 0) -> dict:
    key = jax.random.key(seed)
    ks = jax.random.split(key, 24)
    f32 = jnp.float32
    nrm = lambda k, shape, s: jax.random.normal(k, shape, f32) * s
    return {
        "x_prompt": nrm(ks[0], (BATCH, SEQ, D_MODEL), 1.0),
        "x_sample": nrm(ks[1], (DEC_BATCH, DEC_SEQ, D_MODEL), 1.0),
        "state_pool": nrm(ks[2], (DEPTH, DEC_BATCH, POOL_BUF, POOL_WIDTH), 1.0),
        "state_conv": nrm(ks[3], (DEPTH, DEC_BATCH, CONV_BUF, CONV_WIDTH), 1.0),
        "c_prompt": nrm(ks[4], (BATCH, D_MODEL), 1.0),
        "c_sample": nrm(ks[5], (DEC_BATCH, D_MODEL), 1.0),
        "norm1": 1.0 + nrm(ks[6], (DEPTH, D_MODEL), 0.05),
        "norm2": 1.0 + nrm(ks[7], (DEPTH, D_MODEL), 0.05),
        "w_ada": nrm(ks[8], (DEPTH, D_MODEL, N_MOD * D_MODEL), 0.5 * D_MODEL ** -0.5),
        "b_ada": nrm(ks[9], (DEPTH, N_MOD * D_MODEL), 0.02),
        "w_in": nrm(ks[10], (DEPTH, D_MODEL, N_IN), D_MODEL ** -0.5),
        "w_pool_grp": nrm(ks[11], (DEPTH, N_POOL_GROUPS, POOL_GROUP, POOL_GROUP), POOL_GROUP ** -0.5),
        "pool_scale": 1.0 + nrm(ks[12], (DEPTH, POOL_WIDTH), 0.1),
        "w_conv": nrm(ks[13], (DEPTH, CONV_K, CONV_WIDTH), CONV_K ** -0.5),
        "sgu_norm": 1.0 + nrm(ks[14], (DEPTH, SGU_WIDTH), 0.05),
        "w_sgu": nrm(ks[15], (DEPTH, N_SGU_GROUPS, CHUNK, CHUNK), CHUNK ** -0.5),
        "b_sgu": 1.0 + nrm(ks[16], (DEPTH, N_SGU_GROUPS, CHUNK), 0.1),
        "w_br_pool": nrm(ks[17], (DEPTH, POOL_WIDTH, D_MODEL), POOL_WIDTH ** -0.5),
        "w_br_conv": nrm(ks[18], (DEPTH, CONV_WIDTH, D_MODEL), CONV_WIDTH ** -0.5),
        "w_br_sgu": nrm(ks[19], (DEPTH, SGU_WIDTH, D_MODEL), SGU_WIDTH ** -0.5),
        "w_out": nrm(ks[20], (DEPTH, D_MODEL, D_MODEL), D_MODEL ** -0.5),
        "w_ff1": nrm(ks[21], (DEPTH, D_MODEL, D_FF), D_MODEL ** -0.5),
        "w_ff2": nrm(ks[22], (DEPTH, D_FF, D_MODEL), D_FF ** -0.5),
        "final_norm": 1.0 + nrm(ks[23], (D_MODEL,), 0.05),
    }


def reference(x_prompt, x_sample, state_pool, state_conv, c_prompt, c_sample,
              norm1, norm2, w_ada, b_ada, w_in, w_pool_grp, pool_scale, w_conv,
              sgu_norm, w_sgu, b_sgu, w_br_pool, w_br_conv, w_br_sgu, w_out,
              w_ff1, w_ff2, final_norm):
    Bp, Tp, _ = x_prompt.shape
    Ts = x_sample.shape[1]
    pos_p = jnp.arange(Tp, dtype=jnp.int32)
    pos_s = PAST_LEN + jnp.arange(Ts, dtype=jnp.int32)
    zero_pool = jnp.zeros((Bp, POOL_BUF, POOL_WIDTH), x_prompt.dtype)
    zero_conv = jnp.zeros((Bp, CONV_BUF, CONV_WIDTH), x_prompt.dtype)
    xp, xs = x_prompt, x_sample
    pp, cp, vp, ps, cs_, vs = [], [], [], [], [], []
    for l in range(DEPTH):
        params = (norm1[l], norm2[l], w_ada[l], b_ada[l], w_in[l], w_pool_grp[l],
                  pool_scale[l], w_conv[l], sgu_norm[l], w_sgu[l], b_sgu[l],
                  w_br_pool[l], w_br_conv[l], w_br_sgu[l], w_out[l], w_ff1[l], w_ff2[l])
        xp, a, b, c = layer(xp, c_prompt, zero_pool, zero_conv, pos_p, CHUNK, *params)
        pp.append(a); cp.append(b); vp.append(c)
        xs, a, b, c = layer(xs, c_sample, state_pool[l], state_conv[l], pos_s, Ts, *params)
        ps.append(a); cs_.append(b); vs.append(c)
    y_prompt = rmsnorm(xp, final_norm)
    y_sample = rmsnorm(xs, final_norm)
    return (y_prompt, y_sample, jnp.stack(pp), jnp.stack(cp), jnp.stack(vp),
            jnp.stack(ps), jnp.stack(cs_), jnp.stack(vs))
```

```python
import numpy as np
import concourse.bass as bass
import concourse.mybir as mybir
from concourse.bass_utils import run_bass_kernel_spmd

F32 = mybir.dt.float32
BF16 = mybir.dt.bfloat16
AF = mybir.ActivationFunctionType
ALU = mybir.AluOpType
AX = mybir.AxisListType

D = 2048
NFT = 16
TP = 1024
NSB = 16
TS = 64
T = TP + TS
TH = 544
PH = 512
SH = 32
NSH = 8
L = 4
NCORE = 8
HBLK = [(0, 272), (272, 544)]
FBLK = [(0, 384), (384, 768), (768, 1088)]
N_IN = 10752
OFF_CONV = 512
OFF_U = 512 + 3072
OFF_V = OFF_U + 512
OFF_GATE = OFF_V + 512
EPS = 1e-6
POOL_W = (2, 4, 8, 16)
NSLOT = 6
GELU_C = 1.5957691216057308
ARENA_WORDS = 53200
GR = 128

ENGS = ("pe", "act", "dve", "pool", "sp")
_DSZ = {}


def _dsz(dt):
    k = str(dt)
    if k not in _DSZ:
        _DSZ[k] = 2 if "16" in k else 4
    return _DSZ[k]


def _ivs(ap):
    name = ap.tensor.name
    space = 0 if name == "arena" else 1 + int(name[2:])
    a = ap.ap
    pstep = a[0][0]
    dsz = _dsz(ap.dtype)
    col = ap.offset % pstep if pstep else ap.offset
    free = [(st, cnt) for st, cnt in a[1:] if cnt > 1 and st != 0]
    if not free:
        lo = col * dsz
        return [(space, lo // GR, (lo + dsz + GR - 1) // GR)]
    free.sort(key=lambda x: -abs(x[0]))
    inner_st, inner_cnt = free[-1]
    outer = free[:-1]
    nouter = 1
    for st, cnt in outer:
        nouter *= cnt
    inner_ext = (inner_cnt - 1) * abs(inner_st) + 1
    if nouter > 64 or (outer and abs(outer[-1][0]) * dsz < inner_ext * dsz + GR):
        ext = 1
        for st, cnt in free:
            ext += (cnt - 1) * abs(st)
        lo = col * dsz
        return [(space, lo // GR, (lo + ext * dsz + GR - 1) // GR)]
    offs = [0]
    for st, cnt in outer:
        offs = [o + i * st for o in offs for i in range(cnt)]
    out = []
    for o in offs:
        lo = (col + o) * dsz
        out.append((space, lo // GR, (lo + inner_ext * dsz + GR - 1) // GR))
    return out


class _Op:
    __slots__ = ("eng", "fn", "deps", "sig", "cnt", "isdma", "semkey", "gen", "gi")


class Prog:
    def __init__(self):
        self.ops = []
        self.eng_ops = {e: [] for e in ENGS}
        self.last_w = {}
        self.rd_eng = {}
        self.rd_dma = {}
        self.dma_gen_total = {}
        self.dma_cum = {}
        self.bank_ptr = 0

    def _keys(self, items):
        ks = []
        for it in items:
            if isinstance(it, (str, tuple)):
                ks.append(it)
            else:
                for sp, g0, g1 in _ivs(it):
                    ks.extend([(sp, g) for g in range(g0, g1)])
        return ks

    def _deps(self, op, reads, writes):
        rk, wk = self._keys(reads), self._keys(writes)
        deps = {}

        def add(d, kind):
            if d is op:
                return
            if d.eng == op.eng and not d.isdma and not op.isdma:
                if op.eng == "pe" or kind != "raw":
                    return
            deps[d.gi] = d
        lw, re_, rdm = self.last_w, self.rd_eng, self.rd_dma
        for k in rk:
            w = lw.get(k)
            if w is not None:
                add(w, "raw")
        for k in wk:
            w = lw.get(k)
            if w is not None:
                add(w, "waw")
            r = re_.get(k)
            if r:
                for d in r.values():
                    add(d, "war")
            r = rdm.get(k)
            if r:
                for d in r:
                    add(d, "war")
        op.deps = list(deps.values())
        if op.isdma:
            for k in rk:
                rdm.setdefault(k, []).append(op)
        else:
            e = op.eng
            for k in rk:
                d = re_.get(k)
                if d is None:
                    re_[k] = {e: op}
                else:
                    d[e] = op
        for k in wk:
            lw[k] = op
            if k in re_:
                del re_[k]
            if k in rdm:
                del rdm[k]

    def op(self, eng, fn, reads=(), writes=()):
        o = _Op()
        o.eng, o.fn, o.sig, o.cnt, o.isdma = eng, fn, False, 0, False
        o.semkey = o.gen = None
        o.gi = len(self.ops)
        self._deps(o, reads, writes)
        self.ops.append(o)
        self.eng_ops[eng].append(o)
        return o

    def dma_fn(self, q, fn, reads, writes, semkey, gen=0):
        o = _Op()
        o.eng, o.sig, o.cnt, o.isdma = q, False, 0, True
        o.semkey, o.gen, o.fn = semkey, gen, fn
        o.gi = len(self.ops)
        self._deps(o, reads, writes)
        c = self.dma_cum.get(semkey, 0) + 16
        self.dma_cum[semkey] = c
        self.dma_gen_total[(semkey, gen)] = c
        self.ops.append(o)
        self.eng_ops[q].append(o)
        return o

    def dma(self, q, out, in_, reads=(), writes=(), semkey="misc", gen=0):
        return self.dma_fn(q, lambda e, out=out, in_=in_: e.dma_start(out=out, in_=in_),
                           reads, writes, semkey, gen)

    def bank(self):
        b = self.bank_ptr
        self.bank_ptr = (self.bank_ptr + 1) % 8
        return b

    def emit(self, block, sems, dma_sems):
        for o in self.ops:
            for d in o.deps:
                if not d.isdma:
                    d.sig = True
        for e in ENGS:
            c = 0
            for o in self.eng_ops[e]:
                if not o.isdma and o.sig:
                    c += 1
                    o.cnt = c
        handles = {"pe": "tensor", "act": "scalar", "dve": "vector", "pool": "gpsimd", "sp": "sync"}

        def make(ename):
            def body(eng):
                known = {}
                for o in self.eng_ops[ename]:
                    need = {}
                    for d in o.deps:
                        if d.isdma:
                            key, s = ("d", d.semkey), dma_sems[d.semkey]
                            v = self.dma_gen_total[(d.semkey, d.gen)]
                        else:
                            key, s, v = ("e", d.eng), sems[d.eng], d.cnt
                        if need.get(key, (None, 0))[1] < v:
                            need[key] = (s, v)
                    for key, (s, v) in need.items():
                        if known.get(key, 0) >= v:
                            continue
                        eng.wait_ge(s, v)
                        known[key] = v
                    ins = o.fn(eng)
                    if o.isdma:
                        ins.then_inc(dma_sems[o.semkey], 16)
                    elif o.sig:
                        ins.then_inc(sems[ename], 1)
            return body

        for ename in ENGS:
            getattr(block, handles[ename])(make(ename))


def _bc_last(ap, n):
    shp = list(ap.shape)
    return ap.unsqueeze(len(shp)).to_broadcast(shp + [n])


def _bc_mid(ap, n):
    p, a = ap.shape
    return ap.unsqueeze(1).to_broadcast([p, n, a])


def _r4(ap):
    return ap.rearrange("p (b i) -> p b i", i=4)


def build_nc(NL=L, LW=L):
    nc = bass.Bass("TRN2", target_bir_lowering=False)
    P = Prog()

    def din(name, shape):
        return nc.dram_tensor(name, list(shape), F32, kind="ExternalInput").ap()

    def dout(name, shape):
        return nc.dram_tensor(name, list(shape), F32, kind="ExternalOutput").ap()

    xT = din("xT", [D, T])
    cT = din("cT", [D, 17])
    stp = din("stp", [L, 2, 128, 4, NSH, 15])
    stc = din("stc", [L, 2, 128, 8, NSH, 2])
    w_ada = din("w_ada", [LW, D, 6 * D])
    w_in = din("w_in4", [LW, D, OFF_GATE])
    w_mg = din("w_mg", [LW, D, 16 * 512])
    w_out = din("w_out", [LW, D, D])
    w_ff1 = din("w_ff1", [LW, D, 4 * D])
    w_ff2 = din("w_ff2", [LW, 4 * D, D])
    w_pg = din("w_pool_grp", [L, 4, 128, 128])
    normT = din("normT", [128, 2 * L + 1, 16])
    b_adaT = din("b_adaT", [128, L, 96])
    pscT = din("pscT", [128, L, 4])
    wcvT = din("wcvT", [128, L, 3, 8])
    sgn = din("sgn", [L, 512])
    wsT = din("wsT", [L, 4, 128, 128])
    w4rep = din("w4rep", [L, 4, 32, 32])
    bsg = din("bsg", [L, 512])
    tri = din("tri", [128, 128])
    bdm = din("bdm", [32, 32])
    invc = din("invc", [128, 4, 15])
    par = din("par", [128, 1])
    xhT = din("xhT", [D, 384])
    xt15 = din("xt15", [D, 15])

    yT = dout("yT", [D, T])
    o_pool_p = dout("o_pool_p", [L, 128, 60])
    o_conv_p = dout("o_conv_p", [L, 128, 16])
    o_v_p = dout("o_v_p", [L, 128, 512])
    o_pool_s = dout("o_pool_s", [L, 2, 128, 4, NSH, 15])
    o_conv_s = dout("o_conv_s", [L, 2, 128, 8 * NSH * 2])
    o_v_s = dout("o_v_s", [L, 64, 512])

    from contextlib import ExitStack
    es = ExitStack()
    with es:
        ARENA = es.enter_context(nc.sbuf_tensor("arena", [128, ARENA_WORDS], F32))
        PS = [es.enter_context(nc.psum_tensor("ps%d" % i, [128, 512], F32)) for i in range(8)]
        sems = {e: es.enter_context(nc.semaphore("s_" + e)) for e in ENGS}
        dma_keys = (["init", "outy0", "outy1", "stA", "stB", "o_tail", "o_ocs", "o_vt", "o_psm"]
                    + ["ring%d" % i for i in range(NSLOT)] + ["lay%d" % i for i in range(L)])
        dma_sems = {k: es.enter_context(nc.semaphore("d_" + k)) for k in dma_keys}

        top = [0]

        def view(off, shape, dt=F32):
            n = 1
            for s_ in shape:
                n *= s_
            nb = n * _dsz(dt)
            assert off % 4 == 0 and off + nb <= ARENA_WORDS * 4, (off, nb)
            v = ARENA[:, off // 4:(off + nb + 3) // 4]
            if dt != F32:
                v = v.bitcast(dt)
                v = v[:, 0:n]
            if len(shape) > 1:
                names = " ".join("d%d" % i for i in range(len(shape)))
                v = v.rearrange("p (%s) -> p %s" % (names, names),
                                **{"d%d" % i: shape[i] for i in range(1, len(shape))})
            return v

        def alloc(shape, dt=F32, at=None):
            n = 1
            for s_ in shape:
                n *= s_
            nb = (n * _dsz(dt) + GR - 1) // GR * GR
            if at is None:
                off = top[0]
                top[0] += nb
            else:
                off = at
            return view(off, shape, dt), off, nb

        X, _, _ = alloc([NFT, T])
        HF, h_off, _ = alloc([NFT, T], BF16)
        HH = view(h_off, [NFT, TH], BF16)
        scr_off = h_off + NFT * TH * 2
        AF_, r_off, _ = alloc([8, T], BF16)
        BR = view(r_off, [NFT, TH], BF16)
        RING, _, _ = alloc([NSLOT, 16, 128], BF16)
        MOD, _, _ = alloc([96, 17])
        SC, _, _ = alloc([16, 17], BF16)
        NRM, _, _ = alloc([2 * L + 1, 16])
        BAD, _, _ = alloc([L, 96])
        PSC, _, _ = alloc([L, 4])
        WCV, _, _ = alloc([L, 3, 8])
        SGN, _, _ = alloc([512])
        ONESB, _, _ = alloc([128], BF16)
        ONESF, _, _ = alloc([128])
        BSG, _, _ = alloc([512])
        TRI, _, _ = alloc([128], BF16)
        BDM, _, _ = alloc([32], BF16)
        INVC, _, _ = alloc([4, 15])
        PAR, _, _ = alloc([1])
        WT, _, _ = alloc([4, 128], BF16)
        BD, _, _ = alloc([4, 32], BF16)
        WPG, _, _ = alloc([4, 128], BF16)
        TAIL, _, _ = alloc([76])
        HALO, _, _ = alloc([76])
        CARRY, _, _ = alloc([76])
        HTAIL, _, _ = alloc([76])
        TAILB, _, _ = alloc([76])
        XH, _, _ = alloc([NFT, 384])
        XT15, _, _ = alloc([NFT, 16])
        ZSP, _, _ = alloc([8, NSH, 2])
        OCS, _, _ = alloc([8, NSH, 2])
        SSQ, _, _ = alloc([4])
        HT, _, _ = alloc([16, 16], BF16)
        XS, _, _ = alloc([2, 64])
        T15, _, _ = alloc([16])
        tbase = top[0]
        top[0] = tbase
        SQ, _, _ = alloc([2, T], BF16)
        RSTD, _, _ = alloc([T])
        NTB, _, _ = alloc([2, 384])
        RT, _, _ = alloc([2, 384])
        end_a = top[0]
        top[0] = tbase
        DD, _, _ = alloc([TH], BF16)
        ZF, _, _ = alloc([2 + PH])
        ZS, _, _ = alloc([NSH, 6])
        XC, _, _ = alloc([384])
        BC, _, _ = alloc([TH])
        CV, _, _ = alloc([TH])
        GU, _, _ = alloc([2, 512])
        VT, _, _ = alloc([512])
        VB, _, _ = alloc([512], BF16)
        end_b = top[0]
        top[0] = tbase
        MG, _, _ = alloc([2, TH], BF16)
        GT, _, _ = alloc([2, 3, 384], BF16)
        TT, _, _ = alloc([2, 2, 384])
        end_c = top[0]
        top[0] = tbase
        TQ, _, _ = alloc([16, 16])
        TQB, _, _ = alloc([16, 16], BF16)
        TR, _, _ = alloc([16])
        TZ, _, _ = alloc([16])
        end_d = top[0]
        assert max(end_a, end_b, end_c, end_d) <= ARENA_WORDS * 4, (tbase, end_a, end_b, end_c, end_d)
        so = [scr_off]

        def salloc(shape, dt=F32):
            v, off, nb = alloc(shape, dt, at=so[0])
            so[0] += nb
            assert so[0] <= h_off + NFT * T * 2
            return v
        PF = salloc([4, 15 + PH])
        TA = salloc([15 + PH])
        TB = salloc([15 + PH])
        PSm = salloc([4, NSH, 19])
        SA = salloc([NSH, 19])
        SB_ = salloc([NSH, 19])

        ring_ptr = [0]
        ring_gen = [0] * NSLOT

        def ring_alloc(n=1):
            if n > 1 and ring_ptr[0] % n:
                ring_ptr[0] += n - ring_ptr[0] % n
            if ring_ptr[0] + n > NSLOT:
                ring_ptr[0] = 0
            s = ring_ptr[0]
            ring_ptr[0] = (s + n) % NSLOT
            return s

        def ring_load(s, pieces):
            ring_gen[s] += 1
            for k0, k1, src in pieces:
                P.dma("pool", RING[:, s, k0:k1, :], src, writes=[RING[:, s, k0:k1, :]],
                      semkey="ring%d" % s, gen=ring_gen[s])

        def wcols(w2d, c0, n=1):
            return w2d[:, c0:c0 + n * 128].rearrange("(k p) c -> p k c", p=128)

        def pview(s, n, kk=16):
            return RING[:, s:s + n, :, :].rearrange("p s k c -> p (s k c)").rearrange(
                "p (k c) -> p k c", c=n * 16 * 128 // kk)

        def load_panel(s, n, src, kk=16):
            ring_gen[s] += 1
            pv = pview(s, n, kk)
            P.dma("pool", pv, src, writes=[RING[:, s:s + n, :, :]], semkey="ring%d" % s, gen=ring_gen[s])
            return pv

        def projp(pv, h, k0, k1, src, t0, t1, out_ap):
            c0 = h * 128
            pairs = [(pv[:, k, c0:c0 + 128], src[:, k, t0:t1]) for k in range(k0, k1)]
            return mm_group(out_ap, pairs, [pv[:, k0:k1, c0:c0 + 128], src[:, k0:k1, t0:t1]])

        def mm_group(out_ap, pairs, reads):
            def fn(e, out_ap=out_ap, pairs=pairs):
                n = len(pairs)
                ins = None
                for i, (lt, rh) in enumerate(pairs):
                    ins = e.matmul(out_ap, lhsT=lt, rhs=rh, start=(i == 0), stop=(i == n - 1))
                return ins
            return P.op("pe", fn, reads=reads, writes=[out_ap])

        def proj(slot, k0, k1, src, t0, t1, out_ap):
            pairs = [(RING[:, slot, k, :], src[:, k, t0:t1]) for k in range(k0, k1)]
            return mm_group(out_ap, pairs, [RING[:, slot, k0:k1, :], src[:, k0:k1, t0:t1]])

        def vec(fn, reads, writes):
            return P.op("dve", fn, reads=reads, writes=writes)

        def act(fn, reads, writes):
            return P.op("act", fn, reads=reads, writes=writes)

        def tt(out, in0, in1, op):
            return vec(lambda e: e.tensor_tensor(out=out, in0=in0, in1=in1, op=op), [in0, in1], [out])

        def stt(out, in0, scalar, in1, op0, op1):
            rd = [in0, in1] + ([scalar] if not isinstance(scalar, float) else [])
            return vec(lambda e: e.scalar_tensor_tensor(out=out, in0=in0, scalar=scalar, in1=in1, op0=op0, op1=op1),
                       rd, [out])

        def ts(out, in0, s1, s2, op0, op1=None):
            rd = [in0] + [x for x in (s1, s2) if x is not None and not isinstance(x, float)]
            if op1 is None:
                assert op0 == ALU.mult
                return vec(lambda e: e.tensor_scalar_mul(out=out, in0=in0, scalar1=s1), rd, [out])
            return vec(lambda e: e.tensor_scalar(out=out, in0=in0, scalar1=s1, scalar2=s2, op0=op0, op1=op1),
                       rd, [out])

        def cp(out, in_):
            return vec(lambda e: e.tensor_copy(out=out, in_=in_), [in_], [out])

        def actf(out, in_, func, bias=None, scale=None):
            rd = [in_] + [x for x in (bias, scale) if x is not None and not isinstance(x, float)]
            kw = {}
            if bias is not None:
                kw["bias"] = bias
            if scale is not None:
                kw["scale"] = scale
            return act(lambda e: e.activation(out=out, in_=in_, func=func, **kw), rd, [out])

        st_gen = {}

        def store(key, dst, src, newgen=True):
            if newgen or key not in st_gen:
                st_gen[key] = st_gen.get(key, 0) + 1
            P.dma("sp", dst, src, reads=[src], semkey=key, gen=st_gen[key])

        for ft in range(NFT):
            P.dma("sp", X[:, ft, :], xT[ft * 128:(ft + 1) * 128, :], writes=[X[:, ft, :]], semkey="init")
        CTs = view(tbase, [16, 17])
        CTg = view(tbase + 2048, [16, 17])
        for dst, src in ((CTs, cT.rearrange("(k p) n -> p k n", p=128)), (NRM, normT), (BAD, b_adaT), (PSC, pscT),
                         (WCV, wcvT), (INVC, invc), (PAR, par),
                         (XH, xhT.rearrange("(k p) n -> p k n", p=128)),
                         (XT15[:, :, 0:15], xt15.rearrange("(k p) n -> p k n", p=128))):
            P.dma("sp", dst, src, writes=[dst], semkey="init")
        P.dma("pool", TRI, tri, writes=[TRI], semkey="init")
        P.dma("pool", BDM[0:32, :], bdm, writes=[BDM], semkey="init")
        vec(lambda e: e.memset(ONESB, 1.0), [], [ONESB])
        vec(lambda e: e.memset(ONESF, 1.0), [], [ONESF])
        actf(CTg, CTs, AF.Sigmoid)
        tt(SC, CTs, CTg, ALU.mult)

        def main_segs(t0, t1):
            out = []
            for hf in range(2):
                pa, pb = hf * TH, hf * TH + PH
                sa, sb_ = pb, (hf + 1) * TH
                a, b = max(t0, pa), min(t1, pb)
                if a < b:
                    out.append(("p", a, b, 0))
                a, b = max(t0, sa), min(t1, sb_)
                if a < b:
                    assert (a - sa) % 4 == 0 and (b - a) % 4 == 0
                    out.append(("s", a, b, hf * NSH + (a - sa) // 4))
            return out

        def halo_segs(t0, t1):
            return [("p", t0, t1, 0)]

        def rstd_from(psum_ap, out_ap, d):
            ts(out_ap, psum_ap, 1.0 / d, EPS, ALU.mult, ALU.add)
            actf(out_ap, out_ap, AF.Sqrt)
            vec(lambda e: e.reciprocal(out=out_ap, in_=out_ap), [out_ap], [out_ap])

        def norm_mod(l, which, Hdst, XS_, segfn, xoff, tlen, blks):
            ia, ib = (1, 0) if which == 0 else (4, 3)
            banks = [P.bank() for _ in blks]
            for ft in range(NFT):
                q = ft % 2
                actf(SQ[:, q, 0:tlen], XS_[:, ft, xoff:xoff + tlen], AF.Square)
                for bi, (t0, t1) in enumerate(blks):
                    b = banks[bi]
                    P.op("pe", lambda e, ft=ft, q=q, t0=t0, t1=t1, b=b: e.matmul(
                        PS[b][:, 0:t1 - t0], lhsT=ONESB, rhs=SQ[:, q, t0:t1], start=(ft == 0), stop=(ft == NFT - 1)),
                        reads=[SQ[:, q, t0:t1], ONESB], writes=[PS[b][:, 0:t1 - t0]])
            for bi, (t0, t1) in enumerate(blks):
                rstd_from(PS[banks[bi]][:, 0:t1 - t0], RSTD[:, t0:t1], D)
            cnt = 0
            for ft in range(NFT):
                for kind, a, b, s0 in segfn(xoff, xoff + tlen):
                    n = b - a
                    for c0 in range(0, n, 384):
                        c1 = min(n, c0 + 384)
                        q = cnt % 2
                        cnt += 1
                        la, lb = a - xoff + c0, a - xoff + c1
                        tmp = NTB[:, q, 0:c1 - c0]
                        tt(tmp, XS_[:, ft, a + c0:a + c1], RSTD[:, la:lb], ALU.mult)
                        if kind == "p":
                            actf(Hdst[:, ft, la:lb], tmp, AF.Identity, bias=MOD[:, ib * 16 + ft, 0:1],
                                 scale=MOD[:, ia * 16 + ft, 0:1])
                        else:
                            ns = (c1 - c0) // 4
                            tt(_r4(tmp), _r4(tmp), _bc_last(MOD[:, ia * 16 + ft, 1 + s0:1 + s0 + ns], 4), ALU.mult)
                            tt(_r4(Hdst[:, ft, la:lb]), _r4(tmp), _bc_last(MOD[:, ib * 16 + ft, 1 + s0:1 + s0 + ns], 4),
                               ALU.add)

        def resid_update(gi, o, XS_, segfn, t0, t1, b):
            for kind, a, bnd, s0 in segfn(t0, t1):
                pa, pb = a - t0, bnd - t0
                if kind == "p":
                    stt(XS_[:, o, a:bnd], PS[b][:, pa:pb], MOD[:, gi * 16 + o, 0:1], XS_[:, o, a:bnd], ALU.mult, ALU.add)
                else:
                    ns = (bnd - a) // 4
                    q = o % 2
                    tmp = XS[:, q, 0:bnd - a]
                    tt(_r4(tmp), _r4(PS[b][:, pa:pb]), _bc_last(MOD[:, gi * 16 + o, 1 + s0:1 + s0 + ns], 4), ALU.mult)
                    tt(XS_[:, o, a:bnd], XS_[:, o, a:bnd], tmp, ALU.add)

        def gelu_from_psum(ps_ap, out_ap, rows, n):
            r = slice(0, rows)
            g0, g1 = GU[r, 0, 0:n], GU[r, 1, 0:n]
            actf(g0, ps_ap, AF.Copy)
            actf(g1, ps_ap, AF.Square)
            ts(g1, g1, 0.044715, 1.0, ALU.mult, ALU.add)
            tt(g1, g1, g0, ALU.mult)
            actf(g1, g1, AF.Sigmoid, scale=GELU_C)
            tt(out_ap, g0, g1, ALU.mult)

        def layer_params(l):
            lay = "lay%d" % l
            P.dma("pool", WPG, w_pg[l].rearrange("g c d -> c g d"), writes=[WPG], semkey=lay)
            P.dma("pool", WT, wsT[l].rearrange("g j i -> j g i"), writes=[WT], semkey=lay)
            P.dma("pool", BD[0:32, :, :], w4rep[l].rearrange("g r c -> r g c"), writes=[BD], semkey=lay)
            P.dma("sp", SGN, sgn[l].partition_broadcast(128), writes=[SGN], semkey=lay)
            P.dma("sp", BSG[0:1, :], bsg[l:l + 1, :], writes=[BSG], semkey=lay)
            tt(WT, WT, _bc_mid(TRI, 4), ALU.mult)
            tt(BD[0:32, :, :], BD[0:32, :, :], _bc_mid(BDM[0:32, :], 4), ALU.mult)

        def mods(l):
            j = 0
            while j < 96:
                nj = min(30, 96 - j)
                b = P.bank()
                for jj in range(0, nj, 2):
                    s = ring_alloc(2)
                    pv = load_panel(s, 2, wcols(w_ada[l], (j + jj) * 128, 2))
                    for h in range(2):
                        pairs = [(pv[:, k, h * 128:(h + 1) * 128], SC[:, k, :]) for k in range(16)]
                        mm_group(PS[b][:, (jj + h) * 17:(jj + h + 1) * 17], pairs, [pv[:, :, h * 128:(h + 1) * 128], SC])
                tt(MOD[:, j:j + nj, :], PS[b][:, 0:nj * 17].rearrange("p (j n) -> p j n", n=17),
                   _bc_last(BAD[:, l, j:j + nj], 17), ALU.add)
                j += nj
            for which, grp in ((0, 1), (1, 4)):
                m = MOD[:, grp * 16:(grp + 1) * 16, :]
                stt(m, m, 1.0, _bc_last(NRM[:, 2 * l + which, :], 17), ALU.add, ALU.mult)

        def tail_pass(l):
            xt = XT15[:, :, 0:15] if l == 0 else XH[:, :, 128 * l - 15:128 * l]
            actf(TQB[:, :, 0:15], xt, AF.Square)
            b = P.bank()
            mm_group(PS[b][:, 0:15], [(ONESB, TQB[:, k, 0:15]) for k in range(16)], [ONESB, TQB])
            rstd_from(PS[b][:, 0:15], TR[:, 0:15], D)
            tt(TQ[:, :, 0:15], xt, _bc_mid(TR[:, 0:15], 16), ALU.mult)
            tt(TQ[:, :, 0:15], TQ[:, :, 0:15], _bc_last(MOD[:, 16:32, 0], 15), ALU.mult)
            tt(HT[:, :, 0:15], TQ[:, :, 0:15], _bc_last(MOD[:, 0:16, 0], 15), ALU.add)
            bp = P.bank()
            for g in range(4):
                s = ring_alloc()
                ring_load(s, [(0, 16, wcols(w_in[l], g * 128))])
                proj(s, 0, 16, HT, 0, 15, PS[bp][:, g * 15:(g + 1) * 15])
            bx, bc_ = P.bank(), P.bank()
            for f in range(8):
                s = ring_alloc()
                ring_load(s, [(0, 16, wcols(w_in[l], OFF_CONV + f * 128))])
                proj(s, 0, 16, HT, 13, 15, PS[bx][:, 2 * f:2 * f + 2])
                s = ring_alloc()
                ring_load(s, [(0, 16, wcols(w_in[l], OFF_CONV + 2048 + f * 128))])
                proj(s, 0, 16, HT, 13, 15, PS[bc_][:, 2 * f:2 * f + 2])
            actf(TZ[:, 0:16], PS[bx][:, 0:16], AF.Copy)
            if l < 3:
                actf(HTAIL[:, 0:60], PS[bp][:, 0:60], AF.Copy)
                tt(HTAIL[:, 60:76], PS[bc_][:, 0:16], TZ[:, 0:16], ALU.mult)
            else:
                tt(TZ[:, 0:16], PS[bc_][:, 0:16], TZ[:, 0:16], ALU.mult)
                ts(HALO[:, 60:76], TZ[:, 0:16], PAR[:, 0:1], None, ALU.mult)
                ts(HALO[:, 0:60], PS[bp][:, 0:60], PAR[:, 0:1], None, ALU.mult)

        def token_mix(l, XS_, segfn, xo, ph, sh, pref, tail_dst, hf):
            th = ph + sh
            nsq = sh // 4
            blks = [(0, th)] if th <= 384 else [(0, th // 2), (th // 2, th)]
            if sh:
                stk = "stA" if hf == 0 else "stB"
                P.dma("sp", PSm[:, :, :, 0:15], stp[l, hf], writes=[PSm[:, :, :, 0:15]], semkey=stk, gen=l)
                P.dma("sp", ZSP, stc[l, hf], writes=[ZSP], semkey=stk, gen=l)
            norm_mod(l, 0, HH, XS_, segfn, xo, th, blks)
            for g in range(4):
                if g % 2 == 0:
                    s = ring_alloc(2)
                    pv = load_panel(s, 2, wcols(w_in[l], g * 128, 2))
                for bi, (t0, t1) in enumerate(blks):
                    b = P.bank()
                    projp(pv, g % 2, 0, 16, HH, t0, t1, PS[b][:, 0:t1 - t0])
                    np_ = min(t1, ph) - t0
                    actf(PF[:, g, 15 + t0:15 + t0 + np_], PS[b][:, 0:np_], AF.Copy)
                    if t1 > ph:
                        actf(PSm[:, g, :, 15:19], _r4(PS[b][:, np_:np_ + sh]), AF.Copy)
                cp(PF[:, g, 0:15], pref[:, g * 15:(g + 1) * 15])
            for f in range(8):
                sx, sbb, scc = ring_alloc(), ring_alloc(), ring_alloc()
                ring_load(sx, [(0, 16, wcols(w_in[l], OFF_CONV + f * 128))])
                ring_load(sbb, [(0, 16, wcols(w_in[l], OFF_CONV + 1024 + f * 128))])
                ring_load(scc, [(0, 16, wcols(w_in[l], OFF_CONV + 2048 + f * 128))])
                cp(ZF[:, 0:2], pref[:, 60 + 2 * f:62 + 2 * f])
                if sh:
                    cp(ZS[:, :, 0:2], ZSP[:, f, :, :])
                for bi, (t0, t1) in enumerate(blks):
                    n = t1 - t0
                    np_ = min(t1, ph) - t0
                    bx, bb_, bcx = P.bank(), P.bank(), P.bank()
                    proj(sx, 0, 16, HH, t0, t1, PS[bx][:, 0:n])
                    proj(sbb, 0, 16, HH, t0, t1, PS[bb_][:, 0:n])
                    proj(scc, 0, 16, HH, t0, t1, PS[bcx][:, 0:n])
                    actf(XC[:, 0:n], PS[bx][:, 0:n], AF.Copy)
                    actf(BC[:, t0:t1], PS[bb_][:, 0:n], AF.Copy)
                    tt(ZF[:, 2 + t0:2 + t0 + np_], PS[bcx][:, 0:np_], XC[:, 0:np_], ALU.mult)
                    if t1 > ph:
                        tt(ZS[:, :, 2:6], _r4(PS[bcx][:, np_:np_ + sh]), _r4(XC[:, np_:np_ + sh]), ALU.mult)
                cv = CV[:, 0:ph]
                ts(cv, ZF[:, 0:ph], WCV[:, l, 0, f:f + 1], None, ALU.mult)
                stt(cv, ZF[:, 1:1 + ph], WCV[:, l, 1, f:f + 1], cv, ALU.mult, ALU.add)
                stt(cv, ZF[:, 2:2 + ph], WCV[:, l, 2, f:f + 1], cv, ALU.mult, ALU.add)
                tt(BR[:, 4 + f, 0:ph], cv, BC[:, 0:ph], ALU.mult)
                if sh:
                    cvs = _r4(CV[:, ph:th])
                    ts(cvs, ZS[:, :, 0:4], WCV[:, l, 0, f:f + 1], None, ALU.mult)
                    stt(cvs, ZS[:, :, 1:5], WCV[:, l, 1, f:f + 1], cvs, ALU.mult, ALU.add)
                    stt(cvs, ZS[:, :, 2:6], WCV[:, l, 2, f:f + 1], cvs, ALU.mult, ALU.add)
                    tt(BR[:, 4 + f, ph:th], CV[:, ph:th], BC[:, ph:th], ALU.mult)
                    cp(OCS[:, f, :, :], ZS[:, :, 4:6])
                if hf is None:
                    ts(tail_dst[:, 60 + 2 * f:62 + 2 * f], ZF[:, ph:ph + 2], PAR[:, 0:1], None, ALU.mult)
                else:
                    cp(tail_dst[:, 60 + 2 * f:62 + 2 * f], ZF[:, ph:ph + 2])
            if sh:
                store("o_ocs", o_conv_s[l, hf], OCS.rearrange("p f b r -> p (f b r)"))
            for g in range(4):
                if g % 2 == 0:
                    s = ring_alloc(2)
                    pv = load_panel(s, 2, wcols(w_in[l], OFF_U + g * 128, 2))
                for bi, (t0, t1) in enumerate(blks):
                    b = P.bank()
                    projp(pv, g % 2, 0, 16, HH, t0, t1, PS[b][:, 0:t1 - t0])
                    gelu_from_psum(PS[b][:, 0:t1 - t0], BR[:, 12 + g, t0:t1], 128, t1 - t0)
            s4 = ring_alloc(4)
            pv4 = load_panel(s4, 4, wcols(w_in[l], OFF_V, 4))
            chunks = [(c * 128, 128) for c in range(ph // 128)] + ([(ph, sh)] if sh else [])
            for ci, (c0, cn) in enumerate(chunks):
                b = P.bank()
                q = ci % 2
                is_s = (c0 == ph)
                mm_group(PS[b][0:cn, :], [(HH[:, k, c0:c0 + cn], pv4[:, k, :]) for k in range(16)],
                         [pv4, HH[:, :, c0:c0 + cn]])
                vt = VT[0:cn, :]
                gelu_from_psum(PS[b][0:cn, :], vt, cn, 512)
                actf(GU[0:cn, 1, :], vt, AF.Square)
                vec(lambda e, cn=cn, q=q: e.reduce_sum(out=SSQ[0:cn, q:q + 1], in_=GU[0:cn, 1, :], axis=AX.X),
                    [GU[0:cn, 1, :]], [SSQ[0:cn, q:q + 1]])
                rstd_from(SSQ[0:cn, q:q + 1], SSQ[0:cn, 2 + q:3 + q], 512)
                stt(vt, vt, SSQ[0:cn, 2 + q:3 + q], SGN[0:cn, :], ALU.mult, ALU.mult)
                actf(VB[0:cn, :], vt, AF.Copy)
                if hf == 1 and (not is_s) and c0 == ph - 128:
                    store("o_vt", o_v_p[l], vt)
                if is_s:
                    store("o_vt", o_v_s[l, hf * SH:(hf + 1) * SH, :], vt)
                b2 = P.bank()
                if not is_s:
                    def fn(e, b2=b2):
                        e.matmul(PS[b2][:, :], lhsT=ONESF[0:1, :], rhs=BSG[0:1, :], start=True, stop=False)
                        ins = None
                        for g in range(4):
                            ins = e.matmul(PS[b2][:, g * 128:(g + 1) * 128], lhsT=VB[:, g * 128:(g + 1) * 128],
                                           rhs=WT[:, g, :], start=False, stop=(g == 3))
                        return ins
                    P.op("pe", fn, reads=[VB, WT, BSG, ONESF], writes=[PS[b2][:, :]])
                    dst = BR[:, 12:16, c0:c0 + 128]
                    tt(dst, dst, PS[b2][:, :].rearrange("p (g i) -> p g i", i=128), ALU.mult)
                else:
                    def fn(e, b2=b2):
                        ins = None
                        for g in range(4):
                            o_ = PS[b2][:, g * SH:(g + 1) * SH]
                            e.matmul(_r4(o_), lhsT=ONESF[0:1, :],
                                     rhs=BSG[0:1, g * 128:g * 128 + 4].unsqueeze(1).to_broadcast([1, NSH, 4]),
                                     start=True, stop=False)
                            ins = e.matmul(o_, lhsT=VB[0:SH, g * 128:(g + 1) * 128], rhs=BD[0:SH, g, :],
                                           start=False, stop=True)
                        return ins
                    P.op("pe", fn, reads=[VB[0:SH, :], BD, BSG, ONESF], writes=[PS[b2][:, 0:4 * SH]])
                    dst = BR[:, 12:16, ph:th]
                    tt(dst, dst, PS[b2][:, 0:4 * SH].rearrange("p (g i) -> p g i", i=SH), ALU.mult)
            for g in range(4):
                w = POOL_W[g]
                NP = 15 + ph
                cur, curs = PF[:, g, :], PSm[:, g, :, :]
                k, step = 1, 0
                while k < w:
                    dst, dsts = (TA, TB)[step % 2], (SA, SB_)[step % 2]
                    tt(dst[:, k:NP], cur[:, k:NP], cur[:, 0:NP - k], ALU.add)
                    if sh:
                        tt(dsts[:, :, k:19], curs[:, :, k:19], curs[:, :, 0:19 - k], ALU.add)
                    cur, curs = dst, dsts
                    k *= 2
                    step += 1
                stt(DD[:, 0:ph], cur[:, 15:NP], 1.0 / w, PF[:, g, 15:NP], ALU.mult, ALU.subtract)
                if hf == 0:
                    tmp = T15[:, 0:15]
                    tt(tmp, cur[:, 15:30], INVC[:, g, :], ALU.mult)
                    tt(DD[:, 0:15], tmp, PF[:, g, 15:30], ALU.subtract)
                if hf is None:
                    ts(tail_dst[:, g * 15:(g + 1) * 15], PF[:, g, ph:ph + 15], PAR[:, 0:1], None, ALU.mult)
                else:
                    cp(tail_dst[:, g * 15:(g + 1) * 15], PF[:, g, ph:ph + 15])
                if sh:
                    stt(_r4(DD[:, ph:th]), curs[:, :, 15:19], 1.0 / w, PSm[:, g, :, 15:19], ALU.mult, ALU.subtract)
                for bi, (t0, t1) in enumerate(blks):
                    b = P.bank()
                    mm_group(PS[b][:, 0:t1 - t0], [(WPG[:, g, :], DD[:, t0:t1])], [WPG[:, g, :], DD[:, t0:t1]])
                    actf(BR[:, g, t0:t1], PS[b][:, 0:t1 - t0], AF.Identity, scale=PSC[:, l, g:g + 1])
            if sh:
                store("o_psm", o_pool_s[l, hf], PSm[:, :, :, 4:19])
            if hf == 1:
                store("o_tail", o_pool_p[l], tail_dst[:, 0:60])
                store("o_tail", o_conv_p[l], tail_dst[:, 60:76], newgen=False)

            def merged_tile(f):
                sa = ring_alloc(2)
                pa = load_panel(sa, 2, wcols(w_mg[l], f * 4 * 128, 2))
                sb_ = ring_alloc(2)
                pb = load_panel(sb_, 2, wcols(w_mg[l], (f * 4 + 2) * 128, 2))
                mq = f % 2
                for bi, (t0, t1) in enumerate(blks):
                    n = t1 - t0
                    for bq in range(2):
                        b = P.bank()
                        projp(pa, bq, 0, 16, HH, t0, t1, PS[b][:, 0:n])
                        actf(GT[:, bi, bq, 0:n], PS[b][:, 0:n], AF.Sigmoid)
                for bi, (t0, t1) in enumerate(blks):
                    n = t1 - t0
                    b = P.bank()
                    projp(pb, 0, 0, 16, HH, t0, t1, PS[b][:, 0:n])
                    actf(GT[:, bi, 2, 0:n], PS[b][:, 0:n], AF.Sigmoid)
                    bb = []
                    for bq, (k0, k1) in enumerate(((0, 4), (4, 12), (12, 16))):
                        b = P.bank()
                        bb.append(b)
                        projp(pb, 1, k0, k1, BR, t0, t1, PS[b][:, 0:n])
                    t0_, t1_ = TT[:, bi, 0, 0:n], TT[:, bi, 1, 0:n]
                    tt(t0_, PS[bb[0]][:, 0:n], GT[:, bi, 0, 0:n], ALU.mult)
                    tt(t1_, PS[bb[1]][:, 0:n], GT[:, bi, 1, 0:n], ALU.mult)
                    tt(t0_, t0_, t1_, ALU.add)
                    tt(t1_, PS[bb[2]][:, 0:n], GT[:, bi, 2, 0:n], ALU.mult)
                    tt(MG[:, mq, t0:t1], t0_, t1_, ALU.add)

            def wout_partial(f):
                so_ = ring_alloc()
                ring_load(so_, [(0, 16, w_out[l][f * 128:(f + 1) * 128, :].rearrange("p (o c) -> p o c", c=128))])
                mq = f % 2
                for o in range(NFT):
                    for bi, (t0, t1) in enumerate(blks):
                        b = P.bank()
                        mm_group(PS[b][:, 0:t1 - t0], [(RING[:, so_, o, :], MG[:, mq, t0:t1])],
                                 [RING[:, so_, o, :], MG[:, mq, t0:t1]])
                        resid_update(2, o, XS_, segfn, xo + t0, xo + t1, b)

            merged_tile(0)
            for f in range(NFT):
                if f + 1 < NFT:
                    merged_tile(f + 1)
                wout_partial(f)

        def channel_mlp(l, XS_, segfn, xo, tlen):
            blks = FBLK if tlen == T else [(0, tlen)]
            assert tlen == T or tlen <= 384
            norm_mod(l, 1, HF, XS_, segfn, xo, tlen, blks)
            rr = [0]
            for sl in range(8):
                for jt in range(8):
                    if jt % 2 == 0:
                        s = ring_alloc(2)
                        pv = load_panel(s, 2, wcols(w_ff1[l], sl * 1024 + jt * 128, 2))
                    for bi, (t0, t1) in enumerate(blks):
                        b = P.bank()
                        n = t1 - t0
                        projp(pv, jt % 2, 0, 16, HF, t0, t1, PS[b][:, 0:n])
                        rq = rr[0] % 2
                        rr[0] += 1
                        actf(RT[:, rq, 0:n], PS[b][:, 0:n], AF.Relu)
                        tt(AF_[:, jt, t0:t1], RT[:, rq, 0:n], RT[:, rq, 0:n], ALU.mult)
                w2 = w_ff2[l][sl * 1024:(sl + 1) * 1024, :]
                for o4 in range(0, NFT, 4):
                    s = ring_alloc(2)
                    p8 = load_panel(s, 2, wcols(w2, o4 * 128, 4), kk=8)
                    for oo in range(4):
                        o = o4 + oo
                        for bi, (t0, t1) in enumerate(blks):
                            b = P.bank()
                            pairs = [(p8[:, k, oo * 128:(oo + 1) * 128], AF_[:, k, t0:t1]) for k in range(8)]
                            mm_group(PS[b][:, 0:t1 - t0], pairs, [p8[:, :, oo * 128:(oo + 1) * 128], AF_[:, :, t0:t1]])
                            resid_update(5, o, XS_, segfn, xo + t0, xo + t1, b)

        for l in range(NL):
            layer_params(l)
            mods(l)
            tail_pass(l)
            if l < 3:
                nh = 128 * (3 - l)
                token_mix(l, XH, halo_segs, 384 - nh, nh, 0, HTAIL, HALO, None)
                channel_mlp(l, XH, halo_segs, 384 - nh, nh)
            token_mix(l, X, main_segs, 0, PH, SH, HALO, CARRY, 0)
            token_mix(l, X, main_segs, TH, PH, SH, CARRY, TAILB, 1)
            channel_mlp(l, X, main_segs, 0, T)

        banks = [P.bank() for _ in FBLK]
        for ft in range(NFT):
            q = ft % 2
            actf(SQ[:, q, :], X[:, ft, :], AF.Square)
            for bi, (t0, t1) in enumerate(FBLK):
                b = banks[bi]
                P.op("pe", lambda e, ft=ft, q=q, t0=t0, t1=t1, b=b: e.matmul(
                    PS[b][:, 0:t1 - t0], lhsT=ONESB, rhs=SQ[:, q, t0:t1], start=(ft == 0), stop=(ft == NFT - 1)),
                    reads=[SQ[:, q, t0:t1], ONESB], writes=[PS[b][:, 0:t1 - t0]])
        for bi, (t0, t1) in enumerate(FBLK):
            rstd_from(PS[banks[bi]][:, 0:t1 - t0], RSTD[:, t0:t1], D)
        cnt = 0
        for ft in range(NFT):
            for bi, (t0, t1) in enumerate(FBLK):
                q = cnt % 2
                cnt += 1
                y = NTB[:, q, 0:t1 - t0]
                stt(y, X[:, ft, t0:t1], NRM[:, 2 * L, ft:ft + 1], RSTD[:, t0:t1], ALU.mult, ALU.mult)
                P.dma("sp", yT[ft * 128:(ft + 1) * 128, t0:t1], y, reads=[y], semkey="outy%d" % q, gen=cnt)
        fin = P.op("sp", lambda e: None, reads=[], writes=[])
        lastk = {}
        for o in P.ops:
            if o.isdma and o.semkey.startswith("o"):
                lastk[o.semkey] = o
        fin.deps = list(lastk.values())


        with nc.Block() as block:
            P.emit(block, sems, dma_sems)
    return nc


_NC_CACHE = {}


def _host_inputs(inp):
    f32 = np.float32
    xp, xs = inp["x_prompt"], inp["x_sample"]
    shared = {k: np.ascontiguousarray(inp[k], dtype=f32) for k in
              ("w_ada", "w_out", "w_ff1", "w_ff2", "w_pool_grp")}
    w_in_ = np.asarray(inp["w_in"], f32)
    shared["w_in4"] = np.ascontiguousarray(w_in_[:, :, :OFF_GATE])
    gates = w_in_[:, :, OFF_GATE:].reshape(L, D, 3, 16, 128).transpose(0, 1, 3, 2, 4)
    wbr = np.concatenate([inp["w_br_pool"], inp["w_br_conv"], inp["w_br_sgu"]], 1).astype(f32, copy=False)
    wbr = wbr.reshape(L, D, 16, 1, 128)
    shared["w_mg"] = np.ascontiguousarray(np.concatenate([gates, wbr], 3).reshape(L, D, 16 * 512))
    nrm = np.concatenate([np.stack([inp["norm1"], inp["norm2"]], 1).reshape(2 * L, D),
                          inp["final_norm"][None]], 0)
    shared["normT"] = np.ascontiguousarray(nrm.reshape(2 * L + 1, 16, 128).transpose(2, 0, 1), f32)
    shared["b_adaT"] = np.ascontiguousarray(inp["b_ada"].reshape(L, 96, 128).transpose(2, 0, 1), f32)
    shared["pscT"] = np.ascontiguousarray(inp["pool_scale"].reshape(L, 4, 128).transpose(2, 0, 1), f32)
    shared["wcvT"] = np.ascontiguousarray(inp["w_conv"].reshape(L, 3, 8, 128).transpose(3, 0, 1, 2), f32)
    shared["sgn"] = np.ascontiguousarray(inp["sgu_norm"], f32)
    shared["wsT"] = np.ascontiguousarray(inp["w_sgu"].transpose(0, 1, 3, 2), f32)
    w4 = inp["w_sgu"][:, :, :4, :4].transpose(0, 1, 3, 2)
    shared["w4rep"] = np.ascontiguousarray(np.tile(w4, (1, 1, NSH, NSH)), f32)
    shared["bsg"] = np.ascontiguousarray(inp["b_sgu"].reshape(L, 512), f32)
    jj, ii = np.meshgrid(np.arange(128), np.arange(128), indexing="ij")
    shared["tri"] = (jj <= ii).astype(f32)
    r, c = np.meshgrid(np.arange(32), np.arange(32), indexing="ij")
    shared["bdm"] = ((r // 4 == c // 4) & (r % 4 <= c % 4)).astype(f32)
    maps = []
    for core in range(NCORE):
        b, half = core // 2, core % 2
        m = dict(shared)
        xpc = xp[b, half * TP:(half + 1) * TP]
        xsc = xs[core * NSB:(core + 1) * NSB].reshape(TS, D)
        xcore = np.concatenate([xpc[:PH], xsc[:SH], xpc[PH:], xsc[SH:]], 0)
        m["xT"] = np.ascontiguousarray(xcore.T, f32)
        cc = np.concatenate([inp["c_prompt"][b:b + 1], inp["c_sample"][core * NSB:(core + 1) * NSB]], 0)
        m["cT"] = np.ascontiguousarray(cc.T, f32)
        sp_ = inp["state_pool"][:, core * NSB:(core + 1) * NSB]
        m["stp"] = np.ascontiguousarray(sp_.reshape(L, 2, NSH, 15, 4, 128).transpose(0, 1, 5, 4, 2, 3), f32)
        sc_ = inp["state_conv"][:, core * NSB:(core + 1) * NSB]
        m["stc"] = np.ascontiguousarray(sc_.reshape(L, 2, NSH, 2, 8, 128).transpose(0, 1, 5, 4, 2, 3), f32)
        pos = half * TP + np.arange(15)
        iv = np.stack([1.0 / np.minimum(pos + 1, w) for w in POOL_W], 0)
        m["invc"] = np.ascontiguousarray(np.broadcast_to(iv[None], (128, 4, 15)), f32)
        m["par"] = np.full((128, 1), float(half), f32)
        if half == 1:
            m["xhT"] = np.ascontiguousarray(xp[b, TP - 384:TP].T, f32)
            m["xt15"] = np.ascontiguousarray(xp[b, TP - 384 - 15:TP - 384].T, f32)
        else:
            m["xhT"] = np.zeros((D, 384), f32)
            m["xt15"] = np.zeros((D, 15), f32)
        maps.append(m)
    return maps


def _assemble(res):
    f32 = np.float32
    B, S = 4, 2 * TP
    y_p = np.empty((B, S, D), f32)
    y_s = np.empty((NCORE * NSB, 4, D), f32)
    pool_p = np.empty((L, B, 15, 512), f32)
    conv_p = np.empty((L, B, 2, 1024), f32)
    v_p = np.empty((L, B, 128, 512), f32)
    pool_s = np.empty((L, NCORE * NSB, 15, 512), f32)
    conv_s = np.empty((L, NCORE * NSB, 2, 1024), f32)
    v_s = np.empty((L, NCORE * NSB, 4, 512), f32)
    for core in range(NCORE):
        r = res[core]
        b, half = core // 2, core % 2
        y = np.asarray(r["yT"]).T
        yp = np.concatenate([y[0:PH], y[TH:TH + PH]], 0)
        ys = np.concatenate([y[PH:TH], y[TH + PH:T]], 0)
        y_p[b, half * TP:(half + 1) * TP] = yp
        y_s[core * NSB:(core + 1) * NSB] = ys.reshape(NSB, 4, D)
        sl = slice(core * NSB, (core + 1) * NSB)
        pool_s[:, sl] = np.asarray(r["o_pool_s"]).transpose(0, 1, 4, 5, 3, 2).reshape(L, NSB, 15, 512)
        conv_s[:, sl] = np.asarray(r["o_conv_s"]).reshape(L, 2, 128, 8, NSH, 2).transpose(0, 1, 4, 5, 3, 2).reshape(L, NSB, 2, 1024)
        v_s[:, sl] = np.asarray(r["o_v_s"]).reshape(L, NSB, 4, 512)
        if half == 1:
            pool_p[:, b] = np.asarray(r["o_pool_p"]).reshape(L, 128, 4, 15).transpose(0, 3, 2, 1).reshape(L, 15, 512)
            conv_p[:, b] = np.asarray(r["o_conv_p"]).reshape(L, 128, 8, 2).transpose(0, 3, 2, 1).reshape(L, 2, 1024)
            v_p[:, b] = np.asarray(r["o_v_p"])
    return (y_p, y_s, pool_p, conv_p, v_p, pool_s, conv_s, v_s)


def kernel(**inputs):
    inp = {k: np.asarray(v) for k, v in inputs.items()}
    if "nc" not in _NC_CACHE:
        _NC_CACHE["nc"] = build_nc()
    nc = _NC_CACHE["nc"]
    maps = _host_inputs(inp)
    res = run_bass_kernel_spmd(nc, maps, core_ids=list(range(NCORE)))
    return _assemble(res.results)
```

```python
import numpy as np
import concourse.bass as bass
import concourse.mybir as mybir
from concourse.bass_utils import run_bass_kernel_spmd

F32 = mybir.dt.float32
BF16 = mybir.dt.bfloat16
AF = mybir.ActivationFunctionType
ALU = mybir.AluOpType
AX = mybir.AxisListType

D = 2048
NFT = 16
TP = 1024
NSB = 16
TS = 64
T = TP + TS
TH = 544
PH = 512
SH = 32
NSH = 8
L = 4
NCORE = 8
HBLK = [(0, 272), (272, 544)]
FBLK = [(0, 384), (384, 768), (768, 1088)]
N_IN = 10752
OFF_CONV = 512
OFF_U = 512 + 3072
OFF_V = OFF_U + 512
OFF_GATE = OFF_V + 512
EPS = 1e-6
POOL_W = (2, 4, 8, 16)
NSLOT = 6
GELU_C = 1.5957691216057308
ARENA_WORDS = 53200
GR = 128

ENGS = ("pe", "act", "dve", "pool", "sp")
_DSZ = {}


def _dsz(dt):
    k = str(dt)
    if k not in _DSZ:
        _DSZ[k] = 2 if "16" in k else 4
    return _DSZ[k]


def _ivs(ap):
    name = ap.tensor.name
    space = 0 if name == "arena" else 1 + int(name[2:])
    a = ap.ap
    pstep = a[0][0]
    dsz = _dsz(ap.dtype)
    col = ap.offset % pstep if pstep else ap.offset
    free = [(st, cnt) for st, cnt in a[1:] if cnt > 1 and st != 0]
    if not free:
        lo = col * dsz
        return [(space, lo // GR, (lo + dsz + GR - 1) // GR)]
    free.sort(key=lambda x: -abs(x[0]))
    inner_st, inner_cnt = free[-1]
    outer = free[:-1]
    nouter = 1
    for st, cnt in outer:
        nouter *= cnt
    inner_ext = (inner_cnt - 1) * abs(inner_st) + 1
    if nouter > 64 or (outer and abs(outer[-1][0]) * dsz < inner_ext * dsz + GR):
        ext = 1
        for st, cnt in free:
            ext += (cnt - 1) * abs(st)
        lo = col * dsz
        return [(space, lo // GR, (lo + ext * dsz + GR - 1) // GR)]
    offs = [0]
    for st, cnt in outer:
        offs = [o + i * st for o in offs for i in range(cnt)]
    out = []
    for o in offs:
        lo = (col + o) * dsz
        out.append((space, lo // GR, (lo + inner_ext * dsz + GR - 1) // GR))
    return out


class _Op:
    __slots__ = ("eng", "fn", "deps", "sig", "cnt", "isdma", "semkey", "gen", "gi")


class Prog:
    def __init__(self):
        self.ops = []
        self.eng_ops = {e: [] for e in ENGS}
        self.last_w = {}
        self.rd_eng = {}
        self.rd_dma = {}
        self.dma_gen_total = {}
        self.dma_cum = {}
        self.bank_ptr = 0

    def _keys(self, items):
        ks = []
        for it in items:
            if isinstance(it, (str, tuple)):
                ks.append(it)
            else:
                for sp, g0, g1 in _ivs(it):
                    ks.extend([(sp, g) for g in range(g0, g1)])
        return ks

    def _deps(self, op, reads, writes):
        rk, wk = self._keys(reads), self._keys(writes)
        deps = {}

        def add(d, kind):
            if d is op:
                return
            if d.eng == op.eng and not d.isdma and not op.isdma:
                if op.eng == "pe" or kind != "raw":
                    return
            deps[d.gi] = d
        lw, re_, rdm = self.last_w, self.rd_eng, self.rd_dma
        for k in rk:
            w = lw.get(k)
            if w is not None:
                add(w, "raw")
        for k in wk:
            w = lw.get(k)
            if w is not None:
                add(w, "waw")
            r = re_.get(k)
            if r:
                for d in r.values():
                    add(d, "war")
            r = rdm.get(k)
            if r:
                for d in r:
                    add(d, "war")
        op.deps = list(deps.values())
        if op.isdma:
            for k in rk:
                rdm.setdefault(k, []).append(op)
        else:
            e = op.eng
            for k in rk:
                d = re_.get(k)
                if d is None:
                    re_[k] = {e: op}
                else:
                    d[e] = op
        for k in wk:
            lw[k] = op
            if k in re_:
                del re_[k]
            if k in rdm:
                del rdm[k]

    def op(self, eng, fn, reads=(), writes=()):
        o = _Op()
        o.eng, o.fn, o.sig, o.cnt, o.isdma = eng, fn, False, 0, False
        o.semkey = o.gen = None
        o.gi = len(self.ops)
        self._deps(o, reads, writes)
        self.ops.append(o)
        self.eng_ops[eng].append(o)
        return o

    def dma_fn(self, q, fn, reads, writes, semkey, gen=0):
        o = _Op()
        o.eng, o.sig, o.cnt, o.isdma = q, False, 0, True
        o.semkey, o.gen, o.fn = semkey, gen, fn
        o.gi = len(self.ops)
        self._deps(o, reads, writes)
        c = self.dma_cum.get(semkey, 0) + 16
        self.dma_cum[semkey] = c
        self.dma_gen_total[(semkey, gen)] = c
        self.ops.append(o)
        self.eng_ops[q].append(o)
        return o

    def dma(self, q, out, in_, reads=(), writes=(), semkey="misc", gen=0):
        return self.dma_fn(q, lambda e, out=out, in_=in_: e.dma_start(out=out, in_=in_),
                           reads, writes, semkey, gen)

    def bank(self):
        b = self.bank_ptr
        self.bank_ptr = (self.bank_ptr + 1) % 8
        return b

    def emit(self, block, sems, dma_sems):
        for o in self.ops:
            for d in o.deps:
                if not d.isdma:
                    d.sig = True
        for e in ENGS:
            c = 0
            for o in self.eng_ops[e]:
                if not o.isdma and o.sig:
                    c += 1
                    o.cnt = c
        handles = {"pe": "tensor", "act": "scalar", "dve": "vector", "pool": "gpsimd", "sp": "sync"}

        def make(ename):
            def body(eng):
                known = {}
                for o in self.eng_ops[ename]:
                    need = {}
                    for d in o.deps:
                        if d.isdma:
                            key, s = ("d", d.semkey), dma_sems[d.semkey]
                            v = self.dma_gen_total[(d.semkey, d.gen)]
                        else:
                            key, s, v = ("e", d.eng), sems[d.eng], d.cnt
                        if need.get(key, (None, 0))[1] < v:
                            need[key] = (s, v)
                    for key, (s, v) in need.items():
                        if known.get(key, 0) >= v:
                            continue
                        eng.wait_ge(s, v)
                        known[key] = v
                    ins = o.fn(eng)
                    if o.isdma:
                        ins.then_inc(dma_sems[o.semkey], 16)
                    elif o.sig:
                        ins.then_inc(sems[ename], 1)
            return body

        for ename in ENGS:
            getattr(block, handles[ename])(make(ename))


def _bc_last(ap, n):
    shp = list(ap.shape)
    return ap.unsqueeze(len(shp)).to_broadcast(shp + [n])


def _bc_mid(ap, n):
    p, a = ap.shape
    return ap.unsqueeze(1).to_broadcast([p, n, a])


def _r4(ap):
    return ap.rearrange("p (b i) -> p b i", i=4)


def build_nc(NL=L, LW=L):
    nc = bass.Bass("TRN2", target_bir_lowering=False)
    P = Prog()

    def din(name, shape):
        return nc.dram_tensor(name, list(shape), F32, kind="ExternalInput").ap()

    def dout(name, shape):
        return nc.dram_tensor(name, list(shape), F32, kind="ExternalOutput").ap()

    xT = din("xT", [D, T])
    cT = din("cT", [D, 17])
    stp = din("stp", [L, 2, 128, 4, NSH, 15])
    stc = din("stc", [L, 2, 128, 8, NSH, 2])
    w_ada = din("w_ada", [LW, D, 6 * D])
    w_in = din("w_in4", [LW, D, OFF_GATE])
    w_mg = din("w_mg", [LW, D, 16 * 512])
    w_out = din("w_out", [LW, D, D])
    w_ff1 = din("w_ff1", [LW, D, 4 * D])
    w_ff2 = din("w_ff2", [LW, 4 * D, D])
    w_pg = din("w_pool_grp", [L, 4, 128, 128])
    normT = din("normT", [128, 2 * L + 1, 16])
    b_adaT = din("b_adaT", [128, L, 96])
    pscT = din("pscT", [128, L, 4])
    wcvT = din("wcvT", [128, L, 3, 8])
    sgn = din("sgn", [L, 512])
    wsT = din("wsT", [L, 4, 128, 128])
    w4rep = din("w4rep", [L, 4, 32, 32])
    bsg = din("bsg", [L, 512])
    tri = din("tri", [128, 128])
    bdm = din("bdm", [32, 32])
    invc = din("invc", [128, 4, 15])
    par = din("par", [128, 1])
    xhT = din("xhT", [D, 384])
    xt15 = din("xt15", [D, 15])

    yT = dout("yT", [D, T])
    o_pool_p = dout("o_pool_p", [L, 128, 60])
    o_conv_p = dout("o_conv_p", [L, 128, 16])
    o_v_p = dout("o_v_p", [L, 128, 512])
    o_pool_s = dout("o_pool_s", [L, 2, 128, 4, NSH, 15])
    o_conv_s = dout("o_conv_s", [L, 2, 128, 8 * NSH * 2])
    o_v_s = dout("o_v_s", [L, 64, 512])

    from contextlib import ExitStack
    es = ExitStack()
    with es:
        ARENA = es.enter_context(nc.sbuf_tensor("arena", [128, ARENA_WORDS], F32))
        PS = [es.enter_context(nc.psum_tensor("ps%d" % i, [128, 512], F32)) for i in range(8)]
        sems = {e: es.enter_context(nc.semaphore("s_" + e)) for e in ENGS}
        dma_keys = (["init", "outy0", "outy1", "stA", "stB", "o_tail", "o_ocs", "o_vt", "o_psm"]
                    + ["ring%d" % i for i in range(NSLOT)] + ["lay%d" % i for i in range(L)])
        dma_sems = {k: es.enter_context(nc.semaphore("d_" + k)) for k in dma_keys}

        top = [0]

        def view(off, shape, dt=F32):
            n = 1
            for s_ in shape:
                n *= s_
            nb = n * _dsz(dt)
            assert off % 4 == 0 and off + nb <= ARENA_WORDS * 4, (off, nb)
            v = ARENA[:, off // 4:(off + nb + 3) // 4]
            if dt != F32:
                v = v.bitcast(dt)
                v = v[:, 0:n]
            if len(shape) > 1:
                names = " ".join("d%d" % i for i in range(len(shape)))
                v = v.rearrange("p (%s) -> p %s" % (names, names),
                                **{"d%d" % i: shape[i] for i in range(1, len(shape))})
            return v

        def alloc(shape, dt=F32, at=None):
            n = 1
            for s_ in shape:
                n *= s_
            nb = (n * _dsz(dt) + GR - 1) // GR * GR
            if at is None:
                off = top[0]
                top[0] += nb
            else:
                off = at
            return view(off, shape, dt), off, nb

        X, _, _ = alloc([NFT, T])
        HF, h_off, _ = alloc([NFT, T], BF16)
        HH = view(h_off, [NFT, TH], BF16)
        scr_off = h_off + NFT * TH * 2
        AF_, r_off, _ = alloc([8, T], BF16)
        BR = view(r_off, [NFT, TH], BF16)
        RING, _, _ = alloc([NSLOT, 16, 128], BF16)
        MOD, _, _ = alloc([96, 17])
        SC, _, _ = alloc([16, 17], BF16)
        NRM, _, _ = alloc([2 * L + 1, 16])
        BAD, _, _ = alloc([L, 96])
        PSC, _, _ = alloc([L, 4])
        WCV, _, _ = alloc([L, 3, 8])
        SGN, _, _ = alloc([512])
        ONESB, _, _ = alloc([128], BF16)
        ONESF, _, _ = alloc([128])
        BSG, _, _ = alloc([512])
        TRI, _, _ = alloc([128], BF16)
        BDM, _, _ = alloc([32], BF16)
        INVC, _, _ = alloc([4, 15])
        PAR, _, _ = alloc([1])
        WT, _, _ = alloc([4, 128], BF16)
        BD, _, _ = alloc([4, 32], BF16)
        WPG, _, _ = alloc([4, 128], BF16)
        TAIL, _, _ = alloc([76])
        HALO, _, _ = alloc([76])
        CARRY, _, _ = alloc([76])
        HTAIL, _, _ = alloc([76])
        TAILB, _, _ = alloc([76])
        XH, _, _ = alloc([NFT, 384])
        XT15, _, _ = alloc([NFT, 16])
        ZSP, _, _ = alloc([8, NSH, 2])
        OCS, _, _ = alloc([8, NSH, 2])
        SSQ, _, _ = alloc([4])
        HT, _, _ = alloc([16, 16], BF16)
        XS, _, _ = alloc([2, 64])
        T15, _, _ = alloc([16])
        tbase = top[0]
        top[0] = tbase
        SQ, _, _ = alloc([2, T], BF16)
        RSTD, _, _ = alloc([T])
        NTB, _, _ = alloc([2, 384])
        RT, _, _ = alloc([2, 384])
        end_a = top[0]
        top[0] = tbase
        DD, _, _ = alloc([TH], BF16)
        ZF, _, _ = alloc([2 + PH])
        ZS, _, _ = alloc([NSH, 6])
        XC, _, _ = alloc([384])
        BC, _, _ = alloc([TH])
        CV, _, _ = alloc([TH])
        GU, _, _ = alloc([2, 512])
        VT, _, _ = alloc([512])
        VB, _, _ = alloc([512], BF16)
        end_b = top[0]
        top[0] = tbase
        MG, _, _ = alloc([2, TH], BF16)
        GT, _, _ = alloc([2, 3, 384], BF16)
        TT, _, _ = alloc([2, 2, 384])
        end_c = top[0]
        top[0] = tbase
        TQ, _, _ = alloc([16, 16])
        TQB, _, _ = alloc([16, 16], BF16)
        TR, _, _ = alloc([16])
        TZ, _, _ = alloc([16])
        end_d = top[0]
        assert max(end_a, end_b, end_c, end_d) <= ARENA_WORDS * 4, (tbase, end_a, end_b, end_c, end_d)
        so = [scr_off]

        def salloc(shape, dt=F32):
            v, off, nb = alloc(shape, dt, at=so[0])
            so[0] += nb
            assert so[0] <= h_off + NFT * T * 2
            return v
        PF = salloc([4, 15 + PH])
        TA = salloc([15 + PH])
        TB = salloc([15 + PH])
        PSm = salloc([4, NSH, 19])
        SA = salloc([NSH, 19])
        SB_ = salloc([NSH, 19])

        ring_ptr = [0]
        ring_gen = [0] * NSLOT

        def ring_alloc(n=1):
            if n > 1 and ring_ptr[0] % n:
                ring_ptr[0] += n - ring_ptr[0] % n
            if ring_ptr[0] + n > NSLOT:
                ring_ptr[0] = 0
            s = ring_ptr[0]
            ring_ptr[0] = (s + n) % NSLOT
            return s

        def ring_load(s, pieces):
            ring_gen[s] += 1
            for k0, k1, src in pieces:
                P.dma("pool", RING[:, s, k0:k1, :], src, writes=[RING[:, s, k0:k1, :]],
                      semkey="ring%d" % s, gen=ring_gen[s])

        def wcols(w2d, c0, n=1):
            return w2d[:, c0:c0 + n * 128].rearrange("(k p) c -> p k c", p=128)

        def pview(s, n, kk=16):
            return RING[:, s:s + n, :, :].rearrange("p s k c -> p (s k c)").rearrange(
                "p (k c) -> p k c", c=n * 16 * 128 // kk)

        def load_panel(s, n, src, kk=16):
            ring_gen[s] += 1
            pv = pview(s, n, kk)
            P.dma("pool", pv, src, writes=[RING[:, s:s + n, :, :]], semkey="ring%d" % s, gen=ring_gen[s])
            return pv

        def projp(pv, h, k0, k1, src, t0, t1, out_ap):
            c0 = h * 128
            pairs = [(pv[:, k, c0:c0 + 128], src[:, k, t0:t1]) for k in range(k0, k1)]
            return mm_group(out_ap, pairs, [pv[:, k0:k1, c0:c0 + 128], src[:, k0:k1, t0:t1]])

        def mm_group(out_ap, pairs, reads):
            def fn(e, out_ap=out_ap, pairs=pairs):
                n = len(pairs)
                ins = None
                for i, (lt, rh) in enumerate(pairs):
                    ins = e.matmul(out_ap, lhsT=lt, rhs=rh, start=(i == 0), stop=(i == n - 1))
                return ins
            return P.op("pe", fn, reads=reads, writes=[out_ap])

        def proj(slot, k0, k1, src, t0, t1, out_ap):
            pairs = [(RING[:, slot, k, :], src[:, k, t0:t1]) for k in range(k0, k1)]
            return mm_group(out_ap, pairs, [RING[:, slot, k0:k1, :], src[:, k0:k1, t0:t1]])

        def vec(fn, reads, writes):
            return P.op("dve", fn, reads=reads, writes=writes)

        def act(fn, reads, writes):
            return P.op("act", fn, reads=reads, writes=writes)

        def tt(out, in0, in1, op):
            return vec(lambda e: e.tensor_tensor(out=out, in0=in0, in1=in1, op=op), [in0, in1], [out])

        def stt(out, in0, scalar, in1, op0, op1):
            rd = [in0, in1] + ([scalar] if not isinstance(scalar, float) else [])
            return vec(lambda e: e.scalar_tensor_tensor(out=out, in0=in0, scalar=scalar, in1=in1, op0=op0, op1=op1),
                       rd, [out])

        def ts(out, in0, s1, s2, op0, op1=None):
            rd = [in0] + [x for x in (s1, s2) if x is not None and not isinstance(x, float)]
            if op1 is None:
                assert op0 == ALU.mult
                return vec(lambda e: e.tensor_scalar_mul(out=out, in0=in0, scalar1=s1), rd, [out])
            return vec(lambda e: e.tensor_scalar(out=out, in0=in0, scalar1=s1, scalar2=s2, op0=op0, op1=op1),
                       rd, [out])

        def cp(out, in_):
            return vec(lambda e: e.tensor_copy(out=out, in_=in_), [in_], [out])

        def actf(out, in_, func, bias=None, scale=None):
            rd = [in_] + [x for x in (bias, scale) if x is not None and not isinstance(x, float)]
            kw = {}
            if bias is not None:
                kw["bias"] = bias
            if scale is not None:
                kw["scale"] = scale
            return act(lambda e: e.activation(out=out, in_=in_, func=func, **kw), rd, [out])

        st_gen = {}

        def store(key, dst, src, newgen=True):
            if newgen or key not in st_gen:
                st_gen[key] = st_gen.get(key, 0) + 1
            P.dma("sp", dst, src, reads=[src], semkey=key, gen=st_gen[key])

        for ft in range(NFT):
            P.dma("sp", X[:, ft, :], xT[ft * 128:(ft + 1) * 128, :], writes=[X[:, ft, :]], semkey="init")
        CTs = view(tbase, [16, 17])
        CTg = view(tbase + 2048, [16, 17])
        for dst, src in ((CTs, cT.rearrange("(k p) n -> p k n", p=128)), (NRM, normT), (BAD, b_adaT), (PSC, pscT),
                         (WCV, wcvT), (INVC, invc), (PAR, par),
                         (XH, xhT.rearrange("(k p) n -> p k n", p=128)),
                         (XT15[:, :, 0:15], xt15.rearrange("(k p) n -> p k n", p=128))):
            P.dma("sp", dst, src, writes=[dst], semkey="init")
        P.dma("pool", TRI, tri, writes=[TRI], semkey="init")
        P.dma("pool", BDM[0:32, :], bdm, writes=[BDM], semkey="init")
        vec(lambda e: e.memset(ONESB, 1.0), [], [ONESB])
        vec(lambda e: e.memset(ONESF, 1.0), [], [ONESF])
        actf(CTg, CTs, AF.Sigmoid)
        tt(SC, CTs, CTg, ALU.mult)

        def main_segs(t0, t1):
            out = []
            for hf in range(2):
                pa, pb = hf * TH, hf * TH + PH
                sa, sb_ = pb, (hf + 1) * TH
                a, b = max(t0, pa), min(t1, pb)
                if a < b:
                    out.append(("p", a, b, 0))
                a, b = max(t0, sa), min(t1, sb_)
                if a < b:
                    assert (a - sa) % 4 == 0 and (b - a) % 4 == 0
                    out.append(("s", a, b, hf * NSH + (a - sa) // 4))
            return out

        def halo_segs(t0, t1):
            return [("p", t0, t1, 0)]

        def rstd_from(psum_ap, out_ap, d):
            ts(out_ap, psum_ap, 1.0 / d, EPS, ALU.mult, ALU.add)
            actf(out_ap, out_ap, AF.Sqrt)
            vec(lambda e: e.reciprocal(out=out_ap, in_=out_ap), [out_ap], [out_ap])

        def norm_mod(l, which, Hdst, XS_, segfn, xoff, tlen, blks):
            ia, ib = (1, 0) if which == 0 else (4, 3)
            banks = [P.bank() for _ in blks]
            for ft in range(NFT):
                q = ft % 2
                actf(SQ[:, q, 0:tlen], XS_[:, ft, xoff:xoff + tlen], AF.Square)
                for bi, (t0, t1) in enumerate(blks):
                    b = banks[bi]
                    P.op("pe", lambda e, ft=ft, q=q, t0=t0, t1=t1, b=b: e.matmul(
                        PS[b][:, 0:t1 - t0], lhsT=ONESB, rhs=SQ[:, q, t0:t1], start=(ft == 0), stop=(ft == NFT - 1)),
                        reads=[SQ[:, q, t0:t1], ONESB], writes=[PS[b][:, 0:t1 - t0]])
            for bi, (t0, t1) in enumerate(blks):
                rstd_from(PS[banks[bi]][:, 0:t1 - t0], RSTD[:, t0:t1], D)
            cnt = 0
            for ft in range(NFT):
                for kind, a, b, s0 in segfn(xoff, xoff + tlen):
                    n = b - a
                    for c0 in range(0, n, 384):
                        c1 = min(n, c0 + 384)
                        q = cnt % 2
                        cnt += 1
                        la, lb = a - xoff + c0, a - xoff + c1
                        tmp = NTB[:, q, 0:c1 - c0]
                        tt(tmp, XS_[:, ft, a + c0:a + c1], RSTD[:, la:lb], ALU.mult)
                        if kind == "p":
                            actf(Hdst[:, ft, la:lb], tmp, AF.Identity, bias=MOD[:, ib * 16 + ft, 0:1],
                                 scale=MOD[:, ia * 16 + ft, 0:1])
                        else:
                            ns = (c1 - c0) // 4
                            tt(_r4(tmp), _r4(tmp), _bc_last(MOD[:, ia * 16 + ft, 1 + s0:1 + s0 + ns], 4), ALU.mult)
                            tt(_r4(Hdst[:, ft, la:lb]), _r4(tmp), _bc_last(MOD[:, ib * 16 + ft, 1 + s0:1 + s0 + ns], 4),
                               ALU.add)

        def resid_update(gi, o, XS_, segfn, t0, t1, b):
            for kind, a, bnd, s0 in segfn(t0, t1):
                pa, pb = a - t0, bnd - t0
                if kind == "p":
                    stt(XS_[:, o, a:bnd], PS[b][:, pa:pb], MOD[:, gi * 16 + o, 0:1], XS_[:, o, a:bnd], ALU.mult, ALU.add)
                else:
                    ns = (bnd - a) // 4
                    q = o % 2
                    tmp = XS[:, q, 0:bnd - a]
                    tt(_r4(tmp), _r4(PS[b][:, pa:pb]), _bc_last(MOD[:, gi * 16 + o, 1 + s0:1 + s0 + ns], 4), ALU.mult)
                    tt(XS_[:, o, a:bnd], XS_[:, o, a:bnd], tmp, ALU.add)

        def gelu_from_psum(ps_ap, out_ap, rows, n):
            r = slice(0, rows)
            g0, g1 = GU[r, 0, 0:n], GU[r, 1, 0:n]
            actf(g0, ps_ap, AF.Copy)
            actf(g1, ps_ap, AF.Square)
            ts(g1, g1, 0.044715, 1.0, ALU.mult, ALU.add)
            tt(g1, g1, g0, ALU.mult)
            actf(g1, g1, AF.Sigmoid, scale=GELU_C)
            tt(out_ap, g0, g1, ALU.mult)

        def layer_params(l):
            lay = "lay%d" % l
            P.dma("pool", WPG, w_pg[l].rearrange("g c d -> c g d"), writes=[WPG], semkey=lay)
            P.dma("pool", WT, wsT[l].rearrange("g j i -> j g i"), writes=[WT], semkey=lay)
            P.dma("pool", BD[0:32, :, :], w4rep[l].rearrange("g r c -> r g c"), writes=[BD], semkey=lay)
            P.dma("sp", SGN, sgn[l].partition_broadcast(128), writes=[SGN], semkey=lay)
            P.dma("sp", BSG[0:1, :], bsg[l:l + 1, :], writes=[BSG], semkey=lay)
            tt(WT, WT, _bc_mid(TRI, 4), ALU.mult)
            tt(BD[0:32, :, :], BD[0:32, :, :], _bc_mid(BDM[0:32, :], 4), ALU.mult)

        def mods(l):
            j = 0
            while j < 96:
                nj = min(30, 96 - j)
                b = P.bank()
                for jj in range(0, nj, 2):
                    s = ring_alloc(2)
                    pv = load_panel(s, 2, wcols(w_ada[l], (j + jj) * 128, 2))
                    for h in range(2):
                        pairs = [(pv[:, k, h * 128:(h + 1) * 128], SC[:, k, :]) for k in range(16)]
                        mm_group(PS[b][:, (jj + h) * 17:(jj + h + 1) * 17], pairs, [pv[:, :, h * 128:(h + 1) * 128], SC])
                tt(MOD[:, j:j + nj, :], PS[b][:, 0:nj * 17].rearrange("p (j n) -> p j n", n=17),
                   _bc_last(BAD[:, l, j:j + nj], 17), ALU.add)
                j += nj
            for which, grp in ((0, 1), (1, 4)):
                m = MOD[:, grp * 16:(grp + 1) * 16, :]
                stt(m, m, 1.0, _bc_last(NRM[:, 2 * l + which, :], 17), ALU.add, ALU.mult)

        def tail_pass(l):
            xt = XT15[:, :, 0:15] if l == 0 else XH[:, :, 128 * l - 15:128 * l]
            actf(TQB[:, :, 0:15], xt, AF.Square)
            b = P.bank()
            mm_group(PS[b][:, 0:15], [(ONESB, TQB[:, k, 0:15]) for k in range(16)], [ONESB, TQB])
            rstd_from(PS[b][:, 0:15], TR[:, 0:15], D)
            tt(TQ[:, :, 0:15], xt, _bc_mid(TR[:, 0:15], 16), ALU.mult)
            tt(TQ[:, :, 0:15], TQ[:, :, 0:15], _bc_last(MOD[:, 16:32, 0], 15), ALU.mult)
            tt(HT[:, :, 0:15], TQ[:, :, 0:15], _bc_last(MOD[:, 0:16, 0], 15), ALU.add)
            bp = P.bank()
            for g in range(4):
                s = ring_alloc()
                ring_load(s, [(0, 16, wcols(w_in[l], g * 128))])
                proj(s, 0, 16, HT, 0, 15, PS[bp][:, g * 15:(g + 1) * 15])
            bx, bc_ = P.bank(), P.bank()
            for f in range(8):
                s = ring_alloc()
                ring_load(s, [(0, 16, wcols(w_in[l], OFF_CONV + f * 128))])
                proj(s, 0, 16, HT, 13, 15, PS[bx][:, 2 * f:2 * f + 2])
                s = ring_alloc()
                ring_load(s, [(0, 16, wcols(w_in[l], OFF_CONV + 2048 + f * 128))])
                proj(s, 0, 16, HT, 13, 15, PS[bc_][:, 2 * f:2 * f + 2])
            actf(TZ[:, 0:16], PS[bx][:, 0:16], AF.Copy)
            if l < 3:
                actf(HTAIL[:, 0:60], PS[bp][:, 0:60], AF.Copy)
                tt(HTAIL[:, 60:76], PS[bc_][:, 0:16], TZ[:, 0:16], ALU.mult)
            else:
                tt(TZ[:, 0:16], PS[bc_][:, 0:16], TZ[:, 0:16], ALU.mult)
                ts(HALO[:, 60:76], TZ[:, 0:16], PAR[:, 0:1], None, ALU.mult)
                ts(HALO[:, 0:60], PS[bp][:, 0:60], PAR[:, 0:1], None, ALU.mult)

        def token_mix(l, XS_, segfn, xo, ph, sh, pref, tail_dst, hf):
            th = ph + sh
            nsq = sh // 4
            blks = [(0, th)] if th <= 384 else [(0, th // 2), (th // 2, th)]
            if sh:
                stk = "stA" if hf == 0 else "stB"
                P.dma("sp", PSm[:, :, :, 0:15], stp[l, hf], writes=[PSm[:, :, :, 0:15]], semkey=stk, gen=l)
                P.dma("sp", ZSP, stc[l, hf], writes=[ZSP], semkey=stk, gen=l)
            norm_mod(l, 0, HH, XS_, segfn, xo, th, blks)
            for g in range(4):
                if g % 2 == 0:
                    s = ring_alloc(2)
                    pv = load_panel(s, 2, wcols(w_in[l], g * 128, 2))
                for bi, (t0, t1) in enumerate(blks):
                    b = P.bank()
                    projp(pv, g % 2, 0, 16, HH, t0, t1, PS[b][:, 0:t1 - t0])
                    np_ = min(t1, ph) - t0
                    actf(PF[:, g, 15 + t0:15 + t0 + np_], PS[b][:, 0:np_], AF.Copy)
                    if t1 > ph:
                        actf(PSm[:, g, :, 15:19], _r4(PS[b][:, np_:np_ + sh]), AF.Copy)
                cp(PF[:, g, 0:15], pref[:, g * 15:(g + 1) * 15])
            for f in range(8):
                sx, sbb, scc = ring_alloc(), ring_alloc(), ring_alloc()
                ring_load(sx, [(0, 16, wcols(w_in[l], OFF_CONV + f * 128))])
                ring_load(sbb, [(0, 16, wcols(w_in[l], OFF_CONV + 1024 + f * 128))])
                ring_load(scc, [(0, 16, wcols(w_in[l], OFF_CONV + 2048 + f * 128))])
                cp(ZF[:, 0:2], pref[:, 60 + 2 * f:62 + 2 * f])
                if sh:
                    cp(ZS[:, :, 0:2], ZSP[:, f, :, :])
                for bi, (t0, t1) in enumerate(blks):
                    n = t1 - t0
                    np_ = min(t1, ph) - t0
                    bx, bb_, bcx = P.bank(), P.bank(), P.bank()
                    proj(sx, 0, 16, HH, t0, t1, PS[bx][:, 0:n])
                    proj(sbb, 0, 16, HH, t0, t1, PS[bb_][:, 0:n])
                    proj(scc, 0, 16, HH, t0, t1, PS[bcx][:, 0:n])
                    actf(XC[:, 0:n], PS[bx][:, 0:n], AF.Copy)
                    actf(BC[:, t0:t1], PS[bb_][:, 0:n], AF.Copy)
                    tt(ZF[:, 2 + t0:2 + t0 + np_], PS[bcx][:, 0:np_], XC[:, 0:np_], ALU.mult)
                    if t1 > ph:
                        tt(ZS[:, :, 2:6], _r4(PS[bcx][:, np_:np_ + sh]), _r4(XC[:, np_:np_ + sh]), ALU.mult)
                cv = CV[:, 0:ph]
                ts(cv, ZF[:, 0:ph], WCV[:, l, 0, f:f + 1], None, ALU.mult)
                stt(cv, ZF[:, 1:1 + ph], WCV[:, l, 1, f:f + 1], cv, ALU.mult, ALU.add)
                stt(cv, ZF[:, 2:2 + ph], WCV[:, l, 2, f:f + 1], cv, ALU.mult, ALU.add)
                tt(BR[:, 4 + f, 0:ph], cv, BC[:, 0:ph], ALU.mult)
                if sh:
                    cvs = _r4(CV[:, ph:th])
                    ts(cvs, ZS[:, :, 0:4], WCV[:, l, 0, f:f + 1], None, ALU.mult)
                    stt(cvs, ZS[:, :, 1:5], WCV[:, l, 1, f:f + 1], cvs, ALU.mult, ALU.add)
                    stt(cvs, ZS[:, :, 2:6], WCV[:, l, 2, f:f + 1], cvs, ALU.mult, ALU.add)
                    tt(BR[:, 4 + f, ph:th], CV[:, ph:th], BC[:, ph:th], ALU.mult)
                    cp(OCS[:, f, :, :], ZS[:, :, 4:6])
                if hf is None:
                    ts(tail_dst[:, 60 + 2 * f:62 + 2 * f], ZF[:, ph:ph + 2], PAR[:, 0:1], None, ALU.mult)
                else:
                    cp(tail_dst[:, 60 + 2 * f:62 + 2 * f], ZF[:, ph:ph + 2])
            if sh:
                store("o_ocs", o_conv_s[l, hf], OCS.rearrange("p f b r -> p (f b r)"))
            for g in range(4):
                if g % 2 == 0:
                    s = ring_alloc(2)
                    pv = load_panel(s, 2, wcols(w_in[l], OFF_U + g * 128, 2))
                for bi, (t0, t1) in enumerate(blks):
                    b = P.bank()
                    projp(pv, g % 2, 0, 16, HH, t0, t1, PS[b][:, 0:t1 - t0])
                    gelu_from_psum(PS[b][:, 0:t1 - t0], BR[:, 12 + g, t0:t1], 128, t1 - t0)
            s4 = ring_alloc(4)
            pv4 = load_panel(s4, 4, wcols(w_in[l], OFF_V, 4))
            chunks = [(c * 128, 128) for c in range(ph // 128)] + ([(ph, sh)] if sh else [])
            for ci, (c0, cn) in enumerate(chunks):
                b = P.bank()
                q = ci % 2
                is_s = (c0 == ph)
                mm_group(PS[b][0:cn, :], [(HH[:, k, c0:c0 + cn], pv4[:, k, :]) for k in range(16)],
                         [pv4, HH[:, :, c0:c0 + cn]])
                vt = VT[0:cn, :]
                gelu_from_psum(PS[b][0:cn, :], vt, cn, 512)
                actf(GU[0:cn, 1, :], vt, AF.Square)
                vec(lambda e, cn=cn, q=q: e.reduce_sum(out=SSQ[0:cn, q:q + 1], in_=GU[0:cn, 1, :], axis=AX.X),
                    [GU[0:cn, 1, :]], [SSQ[0:cn, q:q + 1]])
                rstd_from(SSQ[0:cn, q:q + 1], SSQ[0:cn, 2 + q:3 + q], 512)
                stt(vt, vt, SSQ[0:cn, 2 + q:3 + q], SGN[0:cn, :], ALU.mult, ALU.mult)
                actf(VB[0:cn, :], vt, AF.Copy)
                if hf == 1 and (not is_s) and c0 == ph - 128:
                    store("o_vt", o_v_p[l], vt)
                if is_s:
                    store("o_vt", o_v_s[l, hf * SH:(hf + 1) * SH, :], vt)
                b2 = P.bank()
                if not is_s:
                    def fn(e, b2=b2):
                        e.matmul(PS[b2][:, :], lhsT=ONESF[0:1, :], rhs=BSG[0:1, :], start=True, stop=False)
                        ins = None
                        for g in range(4):
                            ins = e.matmul(PS[b2][:, g * 128:(g + 1) * 128], lhsT=VB[:, g * 128:(g + 1) * 128],
                                           rhs=WT[:, g, :], start=False, stop=(g == 3))
                        return ins
                    P.op("pe", fn, reads=[VB, WT, BSG, ONESF], writes=[PS[b2][:, :]])
                    dst = BR[:, 12:16, c0:c0 + 128]
                    tt(dst, dst, PS[b2][:, :].rearrange("p (g i) -> p g i", i=128), ALU.mult)
                else:
                    def fn(e, b2=b2):
                        ins = None
                        for g in range(4):
                            o_ = PS[b2][:, g * SH:(g + 1) * SH]
                            e.matmul(_r4(o_), lhsT=ONESF[0:1, :],
                                     rhs=BSG[0:1, g * 128:g * 128 + 4].unsqueeze(1).to_broadcast([1, NSH, 4]),
                                     start=True, stop=False)
                            ins = e.matmul(o_, lhsT=VB[0:SH, g * 128:(g + 1) * 128], rhs=BD[0:SH, g, :],
                                           start=False, stop=True)
                        return ins
                    P.op("pe", fn, reads=[VB[0:SH, :], BD, BSG, ONESF], writes=[PS[b2][:, 0:4 * SH]])
                    dst = BR[:, 12:16, ph:th]
                    tt(dst, dst, PS[b2][:, 0:4 * SH].rearrange("p (g i) -> p g i", i=SH), ALU.mult)
            for g in range(4):
                w = POOL_W[g]
                NP = 15 + ph
                cur, curs = PF[:, g, :], PSm[:, g, :, :]
                k, step = 1, 0
                while k < w:
                    dst, dsts = (TA, TB)[step % 2], (SA, SB_)[step % 2]
                    tt(dst[:, k:NP], cur[:, k:NP], cur[:, 0:NP - k], ALU.add)
                    if sh:
                        tt(dsts[:, :, k:19], curs[:, :, k:19], curs[:, :, 0:19 - k], ALU.add)
                    cur, curs = dst, dsts
                    k *= 2
                    step += 1
                stt(DD[:, 0:ph], cur[:, 15:NP], 1.0 / w, PF[:, g, 15:NP], ALU.mult, ALU.subtract)
                if hf == 0:
                    tmp = T15[:, 0:15]
                    tt(tmp, cur[:, 15:30], INVC[:, g, :], ALU.mult)
                    tt(DD[:, 0:15], tmp, PF[:, g, 15:30], ALU.subtract)
                if hf is None:
                    ts(tail_dst[:, g * 15:(g + 1) * 15], PF[:, g, ph:ph + 15], PAR[:, 0:1], None, ALU.mult)
                else:
                    cp(tail_dst[:, g * 15:(g + 1) * 15], PF[:, g, ph:ph + 15])
                if sh:
                    stt(_r4(DD[:, ph:th]), curs[:, :, 15:19], 1.0 / w, PSm[:, g, :, 15:19], ALU.mult, ALU.subtract)
                for bi, (t0, t1) in enumerate(blks):
                    b = P.bank()
                    mm_group(PS[b][:, 0:t1 - t0], [(WPG[:, g, :], DD[:, t0:t1])], [WPG[:, g, :], DD[:, t0:t1]])
                    actf(BR[:, g, t0:t1], PS[b][:, 0:t1 - t0], AF.Identity, scale=PSC[:, l, g:g + 1])
            if sh:
                store("o_psm", o_pool_s[l, hf], PSm[:, :, :, 4:19])
            if hf == 1:
                store("o_tail", o_pool_p[l], tail_dst[:, 0:60])
                store("o_tail", o_conv_p[l], tail_dst[:, 60:76], newgen=False)

            def merged_tile(f):
                sa = ring_alloc(2)
                pa = load_panel(sa, 2, wcols(w_mg[l], f * 4 * 128, 2))
                sb_ = ring_alloc(2)
                pb = load_panel(sb_, 2, wcols(w_mg[l], (f * 4 + 2) * 128, 2))
                mq = f % 2
                for bi, (t0, t1) in enumerate(blks):
                    n = t1 - t0
                    for bq in range(2):
                        b = P.bank()
                        projp(pa, bq, 0, 16, HH, t0, t1, PS[b][:, 0:n])
                        actf(GT[:, bi, bq, 0:n], PS[b][:, 0:n], AF.Sigmoid)
                        yield
                for bi, (t0, t1) in enumerate(blks):
                    n = t1 - t0
                    b = P.bank()
                    projp(pb, 0, 0, 16, HH, t0, t1, PS[b][:, 0:n])
                    actf(GT[:, bi, 2, 0:n], PS[b][:, 0:n], AF.Sigmoid)
                    yield
                    bb = []
                    for bq, (k0, k1) in enumerate(((0, 4), (4, 12), (12, 16))):
                        b = P.bank()
                        bb.append(b)
                        projp(pb, 1, k0, k1, BR, t0, t1, PS[b][:, 0:n])
                    t0_, t1_ = TT[:, bi, 0, 0:n], TT[:, bi, 1, 0:n]
                    tt(t0_, PS[bb[0]][:, 0:n], GT[:, bi, 0, 0:n], ALU.mult)
                    tt(t1_, PS[bb[1]][:, 0:n], GT[:, bi, 1, 0:n], ALU.mult)
                    tt(t0_, t0_, t1_, ALU.add)
                    tt(t1_, PS[bb[2]][:, 0:n], GT[:, bi, 2, 0:n], ALU.mult)
                    tt(MG[:, mq, t0:t1], t0_, t1_, ALU.add)
                    yield

            def wout_units(f):
                so_ = ring_alloc()
                ring_load(so_, [(0, 16, w_out[l][f * 128:(f + 1) * 128, :].rearrange("p (o c) -> p o c", c=128))])
                mq = f % 2
                units = []
                for o in range(NFT):
                    for bi, (t0, t1) in enumerate(blks):
                        def unit(o=o, t0=t0, t1=t1):
                            b = P.bank()
                            mm_group(PS[b][:, 0:t1 - t0], [(RING[:, so_, o, :], MG[:, mq, t0:t1])],
                                     [RING[:, so_, o, :], MG[:, mq, t0:t1]])
                            resid_update(2, o, XS_, segfn, xo + t0, xo + t1, b)
                        units.append(unit)
                return units

            for _ in merged_tile(0):
                pass
            for f in range(NFT):
                units = wout_units(f)
                ui = 0
                if f + 1 < NFT:
                    ngrp = 4 * len(blks)
                    per = (len(units) + ngrp - 1) // ngrp
                    for _ in merged_tile(f + 1):
                        for _k in range(per):
                            if ui < len(units):
                                units[ui]()
                                ui += 1
                while ui < len(units):
                    units[ui]()
                    ui += 1

        def channel_mlp(l, XS_, segfn, xo, tlen):
            blks = FBLK if tlen == T else [(0, tlen)]
            assert tlen == T or tlen <= 384
            norm_mod(l, 1, HF, XS_, segfn, xo, tlen, blks)
            rr = [0]
            for sl in range(8):
                for jt in range(8):
                    if jt % 2 == 0:
                        s = ring_alloc(2)
                        pv = load_panel(s, 2, wcols(w_ff1[l], sl * 1024 + jt * 128, 2))
                    for bi, (t0, t1) in enumerate(blks):
                        b = P.bank()
                        n = t1 - t0
                        projp(pv, jt % 2, 0, 16, HF, t0, t1, PS[b][:, 0:n])
                        rq = rr[0] % 2
                        rr[0] += 1
                        actf(RT[:, rq, 0:n], PS[b][:, 0:n], AF.Relu)
                        tt(AF_[:, jt, t0:t1], RT[:, rq, 0:n], RT[:, rq, 0:n], ALU.mult)
                w2 = w_ff2[l][sl * 1024:(sl + 1) * 1024, :]
                for o4 in range(0, NFT, 4):
                    s = ring_alloc(2)
                    p8 = load_panel(s, 2, wcols(w2, o4 * 128, 4), kk=8)
                    for oo in range(4):
                        o = o4 + oo
                        for bi, (t0, t1) in enumerate(blks):
                            b = P.bank()
                            pairs = [(p8[:, k, oo * 128:(oo + 1) * 128], AF_[:, k, t0:t1]) for k in range(8)]
                            mm_group(PS[b][:, 0:t1 - t0], pairs, [p8[:, :, oo * 128:(oo + 1) * 128], AF_[:, :, t0:t1]])
                            resid_update(5, o, XS_, segfn, xo + t0, xo + t1, b)

        for l in range(NL):
            layer_params(l)
            mods(l)
            tail_pass(l)
            if l < 3:
                nh = 128 * (3 - l)
                token_mix(l, XH, halo_segs, 384 - nh, nh, 0, HTAIL, HALO, None)
                channel_mlp(l, XH, halo_segs, 384 - nh, nh)
            token_mix(l, X, main_segs, 0, PH, SH, HALO, CARRY, 0)
            token_mix(l, X, main_segs, TH, PH, SH, CARRY, TAILB, 1)
            channel_mlp(l, X, main_segs, 0, T)

        banks = [P.bank() for _ in FBLK]
        for ft in range(NFT):
            q = ft % 2
            actf(SQ[:, q, :], X[:, ft, :], AF.Square)
            for bi, (t0, t1) in enumerate(FBLK):
                b = banks[bi]
                P.op("pe", lambda e, ft=ft, q=q, t0=t0, t1=t1, b=b: e.matmul(
                    PS[b][:, 0:t1 - t0], lhsT=ONESB, rhs=SQ[:, q, t0:t1], start=(ft == 0), stop=(ft == NFT - 1)),
                    reads=[SQ[:, q, t0:t1], ONESB], writes=[PS[b][:, 0:t1 - t0]])
        for bi, (t0, t1) in enumerate(FBLK):
            rstd_from(PS[banks[bi]][:, 0:t1 - t0], RSTD[:, t0:t1], D)
        cnt = 0
        for ft in range(NFT):
            for bi, (t0, t1) in enumerate(FBLK):
                q = cnt % 2
                cnt += 1
                y = NTB[:, q, 0:t1 - t0]
                stt(y, X[:, ft, t0:t1], NRM[:, 2 * L, ft:ft + 1], RSTD[:, t0:t1], ALU.mult, ALU.mult)
                P.dma("sp", yT[ft * 128:(ft + 1) * 128, t0:t1], y, reads=[y], semkey="outy%d" % q, gen=cnt)
        fin = P.op("sp", lambda e: None, reads=[], writes=[])
        lastk = {}
        for o in P.ops:
            if o.isdma and o.semkey.startswith("o"):
                lastk[o.semkey] = o
        fin.deps = list(lastk.values())


        with nc.Block() as block:
            P.emit(block, sems, dma_sems)
    return nc


_NC_CACHE = {}


def _host_inputs(inp):
    f32 = np.float32
    xp, xs = inp["x_prompt"], inp["x_sample"]
    shared = {k: np.ascontiguousarray(inp[k], dtype=f32) for k in
              ("w_ada", "w_out", "w_ff1", "w_ff2", "w_pool_grp")}
    w_in_ = np.asarray(inp["w_in"], f32)
    shared["w_in4"] = np.ascontiguousarray(w_in_[:, :, :OFF_GATE])
    gates = w_in_[:, :, OFF_GATE:].reshape(L, D, 3, 16, 128).transpose(0, 1, 3, 2, 4)
    wbr = np.concatenate([inp["w_br_pool"], inp["w_br_conv"], inp["w_br_sgu"]], 1).astype(f32, copy=False)
    wbr = wbr.reshape(L, D, 16, 1, 128)
    shared["w_mg"] = np.ascontiguousarray(np.concatenate([gates, wbr], 3).reshape(L, D, 16 * 512))
    nrm = np.concatenate([np.stack([inp["norm1"], inp["norm2"]], 1).reshape(2 * L, D),
                          inp["final_norm"][None]], 0)
    shared["normT"] = np.ascontiguousarray(nrm.reshape(2 * L + 1, 16, 128).transpose(2, 0, 1), f32)
    shared["b_adaT"] = np.ascontiguousarray(inp["b_ada"].reshape(L, 96, 128).transpose(2, 0, 1), f32)
    shared["pscT"] = np.ascontiguousarray(inp["pool_scale"].reshape(L, 4, 128).transpose(2, 0, 1), f32)
    shared["wcvT"] = np.ascontiguousarray(inp["w_conv"].reshape(L, 3, 8, 128).transpose(3, 0, 1, 2), f32)
    shared["sgn"] = np.ascontiguousarray(inp["sgu_norm"], f32)
    shared["wsT"] = np.ascontiguousarray(inp["w_sgu"].transpose(0, 1, 3, 2), f32)
    w4 = inp["w_sgu"][:, :, :4, :4].transpose(0, 1, 3, 2)
    shared["w4rep"] = np.ascontiguousarray(np.tile(w4, (1, 1, NSH, NSH)), f32)
    shared["bsg"] = np.ascontiguousarray(inp["b_sgu"].reshape(L, 512), f32)
    jj, ii = np.meshgrid(np.arange(128), np.arange(128), indexing="ij")
    shared["tri"] = (jj <= ii).astype(f32)
    r, c = np.meshgrid(np.arange(32), np.arange(32), indexing="ij")
    shared["bdm"] = ((r // 4 == c // 4) & (r % 4 <= c % 4)).astype(f32)
    maps = []
    for core in range(NCORE):
        b, half = core // 2, core % 2
        m = dict(shared)
        xpc = xp[b, half * TP:(half + 1) * TP]
        xsc = xs[core * NSB:(core + 1) * NSB].reshape(TS, D)
        xcore = np.concatenate([xpc[:PH], xsc[:SH], xpc[PH:], xsc[SH:]], 0)
        m["xT"] = np.ascontiguousarray(xcore.T, f32)
        cc = np.concatenate([inp["c_prompt"][b:b + 1], inp["c_sample"][core * NSB:(core + 1) * NSB]], 0)
        m["cT"] = np.ascontiguousarray(cc.T, f32)
        sp_ = inp["state_pool"][:, core * NSB:(core + 1) * NSB]
        m["stp"] = np.ascontiguousarray(sp_.reshape(L, 2, NSH, 15, 4, 128).transpose(0, 1, 5, 4, 2, 3), f32)
        sc_ = inp["state_conv"][:, core * NSB:(core + 1) * NSB]
        m["stc"] = np.ascontiguousarray(sc_.reshape(L, 2, NSH, 2, 8, 128).transpose(0, 1, 5, 4, 2, 3), f32)
        pos = half * TP + np.arange(15)
        iv = np.stack([1.0 / np.minimum(pos + 1, w) for w in POOL_W], 0)
        m["invc"] = np.ascontiguousarray(np.broadcast_to(iv[None], (128, 4, 15)), f32)
        m["par"] = np.full((128, 1), float(half), f32)
        if half == 1:
            m["xhT"] = np.ascontiguousarray(xp[b, TP - 384:TP].T, f32)
            m["xt15"] = np.ascontiguousarray(xp[b, TP - 384 - 15:TP - 384].T, f32)
        else:
            m["xhT"] = np.zeros((D, 384), f32)
            m["xt15"] = np.zeros((D, 15), f32)
        maps.append(m)
    return maps


def _assemble(res):
    f32 = np.float32
    B, S = 4, 2 * TP
    y_p = np.empty((B, S, D), f32)
    y_s = np.empty((NCORE * NSB, 4, D), f32)
    pool_p = np.empty((L, B, 15, 512), f32)
    conv_p = np.empty((L, B, 2, 1024), f32)
    v_p = np.empty((L, B, 128, 512), f32)
    pool_s = np.empty((L, NCORE * NSB, 15, 512), f32)
    conv_s = np.empty((L, NCORE * NSB, 2, 1024), f32)
    v_s = np.empty((L, NCORE * NSB, 4, 512), f32)
    for core in range(NCORE):
        r = res[core]
        b, half = core // 2, core % 2
        y = np.asarray(r["yT"]).T
        yp = np.concatenate([y[0:PH], y[TH:TH + PH]], 0)
        ys = np.concatenate([y[PH:TH], y[TH + PH:T]], 0)
        y_p[b, half * TP:(half + 1) * TP] = yp
        y_s[core * NSB:(core + 1) * NSB] = ys.reshape(NSB, 4, D)
        sl = slice(core * NSB, (core + 1) * NSB)
        pool_s[:, sl] = np.asarray(r["o_pool_s"]).transpose(0, 1, 4, 5, 3, 2).reshape(L, NSB, 15, 512)
        conv_s[:, sl] = np.asarray(r["o_conv_s"]).reshape(L, 2, 128, 8, NSH, 2).transpose(0, 1, 4, 5, 3, 2).reshape(L, NSB, 2, 1024)
        v_s[:, sl] = np.asarray(r["o_v_s"]).reshape(L, NSB, 4, 512)
        if half == 1:
            pool_p[:, b] = np.asarray(r["o_pool_p"]).reshape(L, 128, 4, 15).transpose(0, 3, 2, 1).reshape(L, 15, 512)
            conv_p[:, b] = np.asarray(r["o_conv_p"]).reshape(L, 128, 8, 2).transpose(0, 3, 2, 1).reshape(L, 2, 1024)
            v_p[:, b] = np.asarray(r["o_v_p"])
    return (y_p, y_s, pool_p, conv_p, v_p, pool_s, conv_s, v_s)


def kernel(**inputs):
    inp = {k: np.asarray(v) for k, v in inputs.items()}
    if "nc" not in _NC_CACHE:
        _NC_CACHE["nc"] = build_nc()
    nc = _NC_CACHE["nc"]
    maps = _host_inputs(inp)
    res = run_bass_kernel_spmd(nc, maps, core_ids=list(range(NCORE)))
    return _assemble(res.results)
```

```python
import numpy as np
import concourse.bass as bass
import concourse.mybir as mybir
from concourse.bass_utils import run_bass_kernel_spmd

F32 = mybir.dt.float32
BF16 = mybir.dt.bfloat16
AF = mybir.ActivationFunctionType
ALU = mybir.AluOpType
AX = mybir.AxisListType

D = 2048
NFT = 16
TP = 1024
NSB = 16
TS = 64
T = TP + TS
TH = 544
PH = 512
SH = 32
NSH = 8
L = 4
NCORE = 8
HBLK = [(0, 272), (272, 544)]
FBLK = [(0, 384), (384, 768), (768, 1088)]
N_IN = 10752
OFF_CONV = 512
OFF_U = 512 + 3072
OFF_V = OFF_U + 512
OFF_GATE = OFF_V + 512
EPS = 1e-6
POOL_W = (2, 4, 8, 16)
NSLOT = 6
GELU_C = 1.5957691216057308
ARENA_WORDS = 53200
GR = 128

ENGS = ("pe", "act", "dve", "pool", "sp")
_DSZ = {}


def _dsz(dt):
    k = str(dt)
    if k not in _DSZ:
        _DSZ[k] = 2 if "16" in k else 4
    return _DSZ[k]


def _ivs(ap):
    name = ap.tensor.name
    space = 0 if name == "arena" else 1 + int(name[2:])
    a = ap.ap
    pstep = a[0][0]
    dsz = _dsz(ap.dtype)
    col = ap.offset % pstep if pstep else ap.offset
    free = [(st, cnt) for st, cnt in a[1:] if cnt > 1 and st != 0]
    if not free:
        lo = col * dsz
        return [(space, lo // GR, (lo + dsz + GR - 1) // GR)]
    free.sort(key=lambda x: -abs(x[0]))
    inner_st, inner_cnt = free[-1]
    outer = free[:-1]
    nouter = 1
    for st, cnt in outer:
        nouter *= cnt
    inner_ext = (inner_cnt - 1) * abs(inner_st) + 1
    if nouter > 64 or (outer and abs(outer[-1][0]) * dsz < inner_ext * dsz + GR):
        ext = 1
        for st, cnt in free:
            ext += (cnt - 1) * abs(st)
        lo = col * dsz
        return [(space, lo // GR, (lo + ext * dsz + GR - 1) // GR)]
    offs = [0]
    for st, cnt in outer:
        offs = [o + i * st for o in offs for i in range(cnt)]
    out = []
    for o in offs:
        lo = (col + o) * dsz
        out.append((space, lo // GR, (lo + inner_ext * dsz + GR - 1) // GR))
    return out


class _Op:
    __slots__ = ("eng", "fn", "deps", "sig", "cnt", "isdma", "semkey", "gen", "gi")


class Prog:
    def __init__(self):
        self.ops = []
        self.eng_ops = {e: [] for e in ENGS}
        self.last_w = {}
        self.rd_eng = {}
        self.rd_dma = {}
        self.dma_gen_total = {}
        self.dma_cum = {}
        self.bank_ptr = 0

    def _keys(self, items):
        ks = []
        for it in items:
            if isinstance(it, (str, tuple)):
                ks.append(it)
            else:
                for sp, g0, g1 in _ivs(it):
                    ks.extend([(sp, g) for g in range(g0, g1)])
        return ks

    def _deps(self, op, reads, writes):
        rk, wk = self._keys(reads), self._keys(writes)
        deps = {}

        def add(d, kind):
            if d is op:
                return
            if d.eng == op.eng and not d.isdma and not op.isdma:
                if op.eng == "pe" or kind != "raw":
                    return
            deps[d.gi] = d
        lw, re_, rdm = self.last_w, self.rd_eng, self.rd_dma
        for k in rk:
            w = lw.get(k)
            if w is not None:
                add(w, "raw")
        for k in wk:
            w = lw.get(k)
            if w is not None:
                add(w, "waw")
            r = re_.get(k)
            if r:
                for d in r.values():
                    add(d, "war")
            r = rdm.get(k)
            if r:
                for d in r:
                    add(d, "war")
        op.deps = list(deps.values())
        if op.isdma:
            for k in rk:
                rdm.setdefault(k, []).append(op)
        else:
            e = op.eng
            for k in rk:
                d = re_.get(k)
                if d is None:
                    re_[k] = {e: op}
                else:
                    d[e] = op
        for k in wk:
            lw[k] = op
            if k in re_:
                del re_[k]
            if k in rdm:
                del rdm[k]

    def op(self, eng, fn, reads=(), writes=()):
        o = _Op()
        o.eng, o.fn, o.sig, o.cnt, o.isdma = eng, fn, False, 0, False
        o.semkey = o.gen = None
        o.gi = len(self.ops)
        self._deps(o, reads, writes)
        self.ops.append(o)
        self.eng_ops[eng].append(o)
        return o

    def dma_fn(self, q, fn, reads, writes, semkey, gen=0):
        o = _Op()
        o.eng, o.sig, o.cnt, o.isdma = q, False, 0, True
        o.semkey, o.gen, o.fn = semkey, gen, fn
        o.gi = len(self.ops)
        self._deps(o, reads, writes)
        c = self.dma_cum.get(semkey, 0) + 16
        self.dma_cum[semkey] = c
        self.dma_gen_total[(semkey, gen)] = c
        self.ops.append(o)
        self.eng_ops[q].append(o)
        return o

    def dma(self, q, out, in_, reads=(), writes=(), semkey="misc", gen=0):
        return self.dma_fn(q, lambda e, out=out, in_=in_: e.dma_start(out=out, in_=in_),
                           reads, writes, semkey, gen)

    def bank(self):
        b = self.bank_ptr
        self.bank_ptr = (self.bank_ptr + 1) % 8
        return b

    def emit(self, block, sems, dma_sems):
        for o in self.ops:
            for d in o.deps:
                if not d.isdma:
                    d.sig = True
        for e in ENGS:
            c = 0
            for o in self.eng_ops[e]:
                if not o.isdma and o.sig:
                    c += 1
                    o.cnt = c
        handles = {"pe": "tensor", "act": "scalar", "dve": "vector", "pool": "gpsimd", "sp": "sync"}

        def make(ename):
            def body(eng):
                known = {}
                for o in self.eng_ops[ename]:
                    need = {}
                    for d in o.deps:
                        if d.isdma:
                            key, s = ("d", d.semkey), dma_sems[d.semkey]
                            v = self.dma_gen_total[(d.semkey, d.gen)]
                        else:
                            key, s, v = ("e", d.eng), sems[d.eng], d.cnt
                        if need.get(key, (None, 0))[1] < v:
                            need[key] = (s, v)
                    for key, (s, v) in need.items():
                        if known.get(key, 0) >= v:
                            continue
                        eng.wait_ge(s, v)
                        known[key] = v
                    ins = o.fn(eng)
                    if o.isdma:
                        ins.then_inc(dma_sems[o.semkey], 16)
                    elif o.sig:
                        ins.then_inc(sems[ename], 1)
            return body

        for ename in ENGS:
            getattr(block, handles[ename])(make(ename))


def _bc_last(ap, n):
    shp = list(ap.shape)
    return ap.unsqueeze(len(shp)).to_broadcast(shp + [n])


def _bc_mid(ap, n):
    p, a = ap.shape
    return ap.unsqueeze(1).to_broadcast([p, n, a])


def _r4(ap):
    return ap.rearrange("p (b i) -> p b i", i=4)


def build_nc(NL=L, LW=L):
    nc = bass.Bass("TRN2", target_bir_lowering=False)
    P = Prog()

    def din(name, shape):
        return nc.dram_tensor(name, list(shape), F32, kind="ExternalInput").ap()

    def dout(name, shape):
        return nc.dram_tensor(name, list(shape), F32, kind="ExternalOutput").ap()

    xT = din("xT", [D, T])
    cT = din("cT", [D, 17])
    stp = din("stp", [L, 2, 128, 4, NSH, 15])
    stc = din("stc", [L, 2, 128, 8, NSH, 2])
    w_ada = din("w_ada", [LW, D, 6 * D])
    w_in = din("w_in4", [LW, D, OFF_GATE])
    w_mg = din("w_mg", [LW, D, 16 * 512])
    w_out = din("w_out", [LW, D, D])
    w_ff1 = din("w_ff1", [LW, D, 4 * D])
    w_ff2 = din("w_ff2", [LW, 4 * D, D])
    w_pg = din("w_pool_grp", [L, 4, 128, 128])
    normT = din("normT", [128, 2 * L + 1, 16])
    b_adaT = din("b_adaT", [128, L, 96])
    pscT = din("pscT", [128, L, 4])
    wcvT = din("wcvT", [128, L, 3, 8])
    sgn = din("sgn", [L, 512])
    wsT = din("wsT", [L, 4, 128, 128])
    w4rep = din("w4rep", [L, 4, 32, 32])
    bsg = din("bsg", [L, 512])
    tri = din("tri", [128, 128])
    bdm = din("bdm", [32, 32])
    invc = din("invc", [128, 4, 15])
    par = din("par", [128, 1])
    xhT = din("xhT", [D, 384])
    xt15 = din("xt15", [D, 15])

    yT = dout("yT", [D, T])
    o_pool_p = dout("o_pool_p", [L, 128, 60])
    o_conv_p = dout("o_conv_p", [L, 128, 16])
    o_v_p = dout("o_v_p", [L, 128, 512])
    o_pool_s = dout("o_pool_s", [L, 2, 128, 4, NSH, 15])
    o_conv_s = dout("o_conv_s", [L, 2, 128, 8 * NSH * 2])
    o_v_s = dout("o_v_s", [L, 64, 512])

    from contextlib import ExitStack
    es = ExitStack()
    with es:
        ARENA = es.enter_context(nc.sbuf_tensor("arena", [128, ARENA_WORDS], F32))
        PS = [es.enter_context(nc.psum_tensor("ps%d" % i, [128, 512], F32)) for i in range(8)]
        sems = {e: es.enter_context(nc.semaphore("s_" + e)) for e in ENGS}
        dma_keys = (["init", "outy0", "outy1", "stA", "stB", "o_tail", "o_ocs", "o_vt", "o_psm"]
                    + ["ring%d" % i for i in range(NSLOT)] + ["lay%d" % i for i in range(L)])
        dma_sems = {k: es.enter_context(nc.semaphore("d_" + k)) for k in dma_keys}

        top = [0]

        def view(off, shape, dt=F32):
            n = 1
            for s_ in shape:
                n *= s_
            nb = n * _dsz(dt)
            assert off % 4 == 0 and off + nb <= ARENA_WORDS * 4, (off, nb)
            v = ARENA[:, off // 4:(off + nb + 3) // 4]
            if dt != F32:
                v = v.bitcast(dt)
                v = v[:, 0:n]
            if len(shape) > 1:
                names = " ".join("d%d" % i for i in range(len(shape)))
                v = v.rearrange("p (%s) -> p %s" % (names, names),
                                **{"d%d" % i: shape[i] for i in range(1, len(shape))})
            return v

        def alloc(shape, dt=F32, at=None):
            n = 1
            for s_ in shape:
                n *= s_
            nb = (n * _dsz(dt) + GR - 1) // GR * GR
            if at is None:
                off = top[0]
                top[0] += nb
            else:
                off = at
            return view(off, shape, dt), off, nb

        X, _, _ = alloc([NFT, T])
        HF, h_off, _ = alloc([NFT, T], BF16)
        HH = view(h_off, [NFT, TH], BF16)
        scr_off = h_off + NFT * TH * 2
        AF_, r_off, _ = alloc([8, T], BF16)
        BR = view(r_off, [NFT, TH], BF16)
        RING, _, _ = alloc([NSLOT, 16, 128], BF16)
        MOD, _, _ = alloc([96, 17])
        SC, _, _ = alloc([16, 17], BF16)
        NRM, _, _ = alloc([2 * L + 1, 16])
        BAD, _, _ = alloc([L, 96])
        PSC, _, _ = alloc([L, 4])
        WCV, _, _ = alloc([L, 3, 8])
        SGN, _, _ = alloc([512])
        ONESB, _, _ = alloc([128], BF16)
        ONESF, _, _ = alloc([128])
        BSG, _, _ = alloc([512])
        TRI, _, _ = alloc([128], BF16)
        BDM, _, _ = alloc([32], BF16)
        INVC, _, _ = alloc([4, 15])
        PAR, _, _ = alloc([1])
        WT, _, _ = alloc([4, 128], BF16)
        BD, _, _ = alloc([4, 32], BF16)
        WPG, _, _ = alloc([4, 128], BF16)
        TAIL, _, _ = alloc([76])
        HALO, _, _ = alloc([76])
        CARRY, _, _ = alloc([76])
        HTAIL, _, _ = alloc([76])
        TAILB, _, _ = alloc([76])
        XH, _, _ = alloc([NFT, 384])
        XT15, _, _ = alloc([NFT, 16])
        ZSP, _, _ = alloc([8, NSH, 2])
        OCS, _, _ = alloc([8, NSH, 2])
        SSQ, _, _ = alloc([4])
        HT, _, _ = alloc([16, 16], BF16)
        XS, _, _ = alloc([2, 64])
        T15, _, _ = alloc([16])
        tbase = top[0]
        top[0] = tbase
        SQ, _, _ = alloc([2, T], BF16)
        RSTD, _, _ = alloc([T])
        NTB, _, _ = alloc([2, 384])
        RT, _, _ = alloc([2, 384])
        end_a = top[0]
        top[0] = tbase
        DD, _, _ = alloc([TH], BF16)
        ZF, _, _ = alloc([2 + PH])
        ZS, _, _ = alloc([NSH, 6])
        XC, _, _ = alloc([384])
        BC, _, _ = alloc([TH])
        CV, _, _ = alloc([TH])
        GU, _, _ = alloc([2, 512])
        VT, _, _ = alloc([512])
        VB, _, _ = alloc([512], BF16)
        end_b = top[0]
        top[0] = tbase
        MG, _, _ = alloc([2, TH], BF16)
        GT, _, _ = alloc([2, 3, 384], BF16)
        TT, _, _ = alloc([2, 2, 384])
        end_c = top[0]
        top[0] = tbase
        TQ, _, _ = alloc([16, 16])
        TQB, _, _ = alloc([16, 16], BF16)
        TR, _, _ = alloc([16])
        TZ, _, _ = alloc([16])
        end_d = top[0]
        assert max(end_a, end_b, end_c, end_d) <= ARENA_WORDS * 4, (tbase, end_a, end_b, end_c, end_d)
        so = [scr_off]

        def salloc(shape, dt=F32):
            v, off, nb = alloc(shape, dt, at=so[0])
            so[0] += nb
            assert so[0] <= h_off + NFT * T * 2
            return v
        PF = salloc([4, 15 + PH])
        TA = salloc([15 + PH])
        TB = salloc([15 + PH])
        PSm = salloc([4, NSH, 19])
        SA = salloc([NSH, 19])
        SB_ = salloc([NSH, 19])

        ring_ptr = [0]
        ring_gen = [0] * NSLOT

        def ring_alloc(n=1):
            if n > 1 and ring_ptr[0] % n:
                ring_ptr[0] += n - ring_ptr[0] % n
            if ring_ptr[0] + n > NSLOT:
                ring_ptr[0] = 0
            s = ring_ptr[0]
            ring_ptr[0] = (s + n) % NSLOT
            return s

        def ring_load(s, pieces):
            ring_gen[s] += 1
            for k0, k1, src in pieces:
                P.dma("pool", RING[:, s, k0:k1, :], src, writes=[RING[:, s, k0:k1, :]],
                      semkey="ring%d" % s, gen=ring_gen[s])

        def wcols(w2d, c0, n=1):
            return w2d[:, c0:c0 + n * 128].rearrange("(k p) c -> p k c", p=128)

        def pview(s, n, kk=16):
            return RING[:, s:s + n, :, :].rearrange("p s k c -> p (s k c)").rearrange(
                "p (k c) -> p k c", c=n * 16 * 128 // kk)

        def load_panel(s, n, src, kk=16):
            ring_gen[s] += 1
            pv = pview(s, n, kk)
            P.dma("pool", pv, src, writes=[RING[:, s:s + n, :, :]], semkey="ring%d" % s, gen=ring_gen[s])
            return pv

        def projp(pv, h, k0, k1, src, t0, t1, out_ap):
            c0 = h * 128
            pairs = [(pv[:, k, c0:c0 + 128], src[:, k, t0:t1]) for k in range(k0, k1)]
            return mm_group(out_ap, pairs, [pv[:, k0:k1, c0:c0 + 128], src[:, k0:k1, t0:t1]])

        def mm_group(out_ap, pairs, reads):
            def fn(e, out_ap=out_ap, pairs=pairs):
                n = len(pairs)
                ins = None
                for i, (lt, rh) in enumerate(pairs):
                    ins = e.matmul(out_ap, lhsT=lt, rhs=rh, start=(i == 0), stop=(i == n - 1))
                return ins
            return P.op("pe", fn, reads=reads, writes=[out_ap])

        def proj(slot, k0, k1, src, t0, t1, out_ap):
            pairs = [(RING[:, slot, k, :], src[:, k, t0:t1]) for k in range(k0, k1)]
            return mm_group(out_ap, pairs, [RING[:, slot, k0:k1, :], src[:, k0:k1, t0:t1]])

        def vec(fn, reads, writes):
            return P.op("dve", fn, reads=reads, writes=writes)

        def act(fn, reads, writes):
            return P.op("act", fn, reads=reads, writes=writes)

        def tt(out, in0, in1, op):
            return vec(lambda e: e.tensor_tensor(out=out, in0=in0, in1=in1, op=op), [in0, in1], [out])

        def stt(out, in0, scalar, in1, op0, op1):
            rd = [in0, in1] + ([scalar] if not isinstance(scalar, float) else [])
            return vec(lambda e: e.scalar_tensor_tensor(out=out, in0=in0, scalar=scalar, in1=in1, op0=op0, op1=op1),
                       rd, [out])

        def ts(out, in0, s1, s2, op0, op1=None):
            rd = [in0] + [x for x in (s1, s2) if x is not None and not isinstance(x, float)]
            if op1 is None:
                assert op0 == ALU.mult
                return vec(lambda e: e.tensor_scalar_mul(out=out, in0=in0, scalar1=s1), rd, [out])
            return vec(lambda e: e.tensor_scalar(out=out, in0=in0, scalar1=s1, scalar2=s2, op0=op0, op1=op1),
                       rd, [out])

        def cp(out, in_):
            return vec(lambda e: e.tensor_copy(out=out, in_=in_), [in_], [out])

        def actf(out, in_, func, bias=None, scale=None):
            rd = [in_] + [x for x in (bias, scale) if x is not None and not isinstance(x, float)]
            kw = {}
            if bias is not None:
                kw["bias"] = bias
            if scale is not None:
                kw["scale"] = scale
            return act(lambda e: e.activation(out=out, in_=in_, func=func, **kw), rd, [out])

        st_gen = {}

        def store(key, dst, src, newgen=True):
            if newgen or key not in st_gen:
                st_gen[key] = st_gen.get(key, 0) + 1
            P.dma("sp", dst, src, reads=[src], semkey=key, gen=st_gen[key])

        for ft in range(NFT):
            P.dma("sp", X[:, ft, :], xT[ft * 128:(ft + 1) * 128, :], writes=[X[:, ft, :]], semkey="init")
        CTs = view(tbase, [16, 17])
        CTg = view(tbase + 2048, [16, 17])
        for dst, src in ((CTs, cT.rearrange("(k p) n -> p k n", p=128)), (NRM, normT), (BAD, b_adaT), (PSC, pscT),
                         (WCV, wcvT), (INVC, invc), (PAR, par),
                         (XH, xhT.rearrange("(k p) n -> p k n", p=128)),
                         (XT15[:, :, 0:15], xt15.rearrange("(k p) n -> p k n", p=128))):
            P.dma("sp", dst, src, writes=[dst], semkey="init")
        P.dma("pool", TRI, tri, writes=[TRI], semkey="init")
        P.dma("pool", BDM[0:32, :], bdm, writes=[BDM], semkey="init")
        vec(lambda e: e.memset(ONESB, 1.0), [], [ONESB])
        vec(lambda e: e.memset(ONESF, 1.0), [], [ONESF])
        actf(CTg, CTs, AF.Sigmoid)
        tt(SC, CTs, CTg, ALU.mult)

        def main_segs(t0, t1):
            out = []
            for hf in range(2):
                pa, pb = hf * TH, hf * TH + PH
                sa, sb_ = pb, (hf + 1) * TH
                a, b = max(t0, pa), min(t1, pb)
                if a < b:
                    out.append(("p", a, b, 0))
                a, b = max(t0, sa), min(t1, sb_)
                if a < b:
                    assert (a - sa) % 4 == 0 and (b - a) % 4 == 0
                    out.append(("s", a, b, hf * NSH + (a - sa) // 4))
            return out

        def halo_segs(t0, t1):
            return [("p", t0, t1, 0)]

        def rstd_from(psum_ap, out_ap, d):
            ts(out_ap, psum_ap, 1.0 / d, EPS, ALU.mult, ALU.add)
            actf(out_ap, out_ap, AF.Sqrt)
            vec(lambda e: e.reciprocal(out=out_ap, in_=out_ap), [out_ap], [out_ap])

        def norm_mod(l, which, Hdst, XS_, segfn, xoff, tlen, blks):
            ia, ib = (1, 0) if which == 0 else (4, 3)
            banks = [P.bank() for _ in blks]
            for ft in range(NFT):
                q = ft % 2
                actf(SQ[:, q, 0:tlen], XS_[:, ft, xoff:xoff + tlen], AF.Square)
                for bi, (t0, t1) in enumerate(blks):
                    b = banks[bi]
                    P.op("pe", lambda e, ft=ft, q=q, t0=t0, t1=t1, b=b: e.matmul(
                        PS[b][:, 0:t1 - t0], lhsT=ONESB, rhs=SQ[:, q, t0:t1], start=(ft == 0), stop=(ft == NFT - 1)),
                        reads=[SQ[:, q, t0:t1], ONESB], writes=[PS[b][:, 0:t1 - t0]])
            for bi, (t0, t1) in enumerate(blks):
                rstd_from(PS[banks[bi]][:, 0:t1 - t0], RSTD[:, t0:t1], D)
            cnt = 0
            for ft in range(NFT):
                for kind, a, b, s0 in segfn(xoff, xoff + tlen):
                    n = b - a
                    for c0 in range(0, n, 384):
                        c1 = min(n, c0 + 384)
                        q = cnt % 2
                        cnt += 1
                        la, lb = a - xoff + c0, a - xoff + c1
                        tmp = NTB[:, q, 0:c1 - c0]
                        tt(tmp, XS_[:, ft, a + c0:a + c1], RSTD[:, la:lb], ALU.mult)
                        if kind == "p":
                            actf(Hdst[:, ft, la:lb], tmp, AF.Identity, bias=MOD[:, ib * 16 + ft, 0:1],
                                 scale=MOD[:, ia * 16 + ft, 0:1])
                        else:
                            ns = (c1 - c0) // 4
                            tt(_r4(tmp), _r4(tmp), _bc_last(MOD[:, ia * 16 + ft, 1 + s0:1 + s0 + ns], 4), ALU.mult)
                            tt(_r4(Hdst[:, ft, la:lb]), _r4(tmp), _bc_last(MOD[:, ib * 16 + ft, 1 + s0:1 + s0 + ns], 4),
                               ALU.add)

        def resid_update(gi, o, XS_, segfn, t0, t1, b):
            for kind, a, bnd, s0 in segfn(t0, t1):
                pa, pb = a - t0, bnd - t0
                if kind == "p":
                    stt(XS_[:, o, a:bnd], PS[b][:, pa:pb], MOD[:, gi * 16 + o, 0:1], XS_[:, o, a:bnd], ALU.mult, ALU.add)
                else:
                    ns = (bnd - a) // 4
                    q = o % 2
                    tmp = XS[:, q, 0:bnd - a]
                    tt(_r4(tmp), _r4(PS[b][:, pa:pb]), _bc_last(MOD[:, gi * 16 + o, 1 + s0:1 + s0 + ns], 4), ALU.mult)
                    tt(XS_[:, o, a:bnd], XS_[:, o, a:bnd], tmp, ALU.add)

        def gelu_from_psum(ps_ap, out_ap, rows, n):
            r = slice(0, rows)
            g0, g1 = GU[r, 0, 0:n], GU[r, 1, 0:n]
            actf(g0, ps_ap, AF.Copy)
            actf(g1, ps_ap, AF.Square)
            ts(g1, g1, 0.044715, 1.0, ALU.mult, ALU.add)
            tt(g1, g1, g0, ALU.mult)
            actf(g1, g1, AF.Sigmoid, scale=GELU_C)
            tt(out_ap, g0, g1, ALU.mult)

        def layer_params(l):
            lay = "lay%d" % l
            P.dma("pool", WPG, w_pg[l].rearrange("g c d -> c g d"), writes=[WPG], semkey=lay)
            P.dma("pool", WT, wsT[l].rearrange("g j i -> j g i"), writes=[WT], semkey=lay)
            P.dma("pool", BD[0:32, :, :], w4rep[l].rearrange("g r c -> r g c"), writes=[BD], semkey=lay)
            P.dma("sp", SGN, sgn[l].partition_broadcast(128), writes=[SGN], semkey=lay)
            P.dma("sp", BSG[0:1, :], bsg[l:l + 1, :], writes=[BSG], semkey=lay)
            tt(WT, WT, _bc_mid(TRI, 4), ALU.mult)
            tt(BD[0:32, :, :], BD[0:32, :, :], _bc_mid(BDM[0:32, :], 4), ALU.mult)

        def mods(l):
            j = 0
            while j < 96:
                nj = min(30, 96 - j)
                b = P.bank()
                for jj in range(0, nj, 2):
                    s = ring_alloc(2)
                    pv = load_panel(s, 2, wcols(w_ada[l], (j + jj) * 128, 2))
                    for h in range(2):
                        pairs = [(pv[:, k, h * 128:(h + 1) * 128], SC[:, k, :]) for k in range(16)]
                        mm_group(PS[b][:, (jj + h) * 17:(jj + h + 1) * 17], pairs, [pv[:, :, h * 128:(h + 1) * 128], SC])
                tt(MOD[:, j:j + nj, :], PS[b][:, 0:nj * 17].rearrange("p (j n) -> p j n", n=17),
                   _bc_last(BAD[:, l, j:j + nj], 17), ALU.add)
                j += nj
            for which, grp in ((0, 1), (1, 4)):
                m = MOD[:, grp * 16:(grp + 1) * 16, :]
                stt(m, m, 1.0, _bc_last(NRM[:, 2 * l + which, :], 17), ALU.add, ALU.mult)

        def tail_pass(l):
            xt = XT15[:, :, 0:15] if l == 0 else XH[:, :, 128 * l - 15:128 * l]
            actf(TQB[:, :, 0:15], xt, AF.Square)
            b = P.bank()
            mm_group(PS[b][:, 0:15], [(ONESB, TQB[:, k, 0:15]) for k in range(16)], [ONESB, TQB])
            rstd_from(PS[b][:, 0:15], TR[:, 0:15], D)
            tt(TQ[:, :, 0:15], xt, _bc_mid(TR[:, 0:15], 16), ALU.mult)
            tt(TQ[:, :, 0:15], TQ[:, :, 0:15], _bc_last(MOD[:, 16:32, 0], 15), ALU.mult)
            tt(HT[:, :, 0:15], TQ[:, :, 0:15], _bc_last(MOD[:, 0:16, 0], 15), ALU.add)
            bp = P.bank()
            for g in range(4):
                s = ring_alloc()
                ring_load(s, [(0, 16, wcols(w_in[l], g * 128))])
                proj(s, 0, 16, HT, 0, 15, PS[bp][:, g * 15:(g + 1) * 15])
            bx, bc_ = P.bank(), P.bank()
            for f in range(8):
                s = ring_alloc()
                ring_load(s, [(0, 16, wcols(w_in[l], OFF_CONV + f * 128))])
                proj(s, 0, 16, HT, 13, 15, PS[bx][:, 2 * f:2 * f + 2])
                s = ring_alloc()
                ring_load(s, [(0, 16, wcols(w_in[l], OFF_CONV + 2048 + f * 128))])
                proj(s, 0, 16, HT, 13, 15, PS[bc_][:, 2 * f:2 * f + 2])
            actf(TZ[:, 0:16], PS[bx][:, 0:16], AF.Copy)
            if l < 3:
                actf(HTAIL[:, 0:60], PS[bp][:, 0:60], AF.Copy)
                tt(HTAIL[:, 60:76], PS[bc_][:, 0:16], TZ[:, 0:16], ALU.mult)
            else:
                tt(TZ[:, 0:16], PS[bc_][:, 0:16], TZ[:, 0:16], ALU.mult)
                ts(HALO[:, 60:76], TZ[:, 0:16], PAR[:, 0:1], None, ALU.mult)
                ts(HALO[:, 0:60], PS[bp][:, 0:60], PAR[:, 0:1], None, ALU.mult)

        def token_mix(l, XS_, segfn, xo, ph, sh, pref, tail_dst, hf):
            th = ph + sh
            nsq = sh // 4
            blks = [(0, th)] if th <= 384 else [(0, th // 2), (th // 2, th)]
            if sh:
                stk = "stA" if hf == 0 else "stB"
                P.dma("sp", PSm[:, :, :, 0:15], stp[l, hf], writes=[PSm[:, :, :, 0:15]], semkey=stk, gen=l)
                P.dma("sp", ZSP, stc[l, hf], writes=[ZSP], semkey=stk, gen=l)
            norm_mod(l, 0, HH, XS_, segfn, xo, th, blks)
            for g in range(4):
                if g % 2 == 0:
                    s = ring_alloc(2)
                    pv = load_panel(s, 2, wcols(w_in[l], g * 128, 2))
                for bi, (t0, t1) in enumerate(blks):
                    b = P.bank()
                    projp(pv, g % 2, 0, 16, HH, t0, t1, PS[b][:, 0:t1 - t0])
                    np_ = min(t1, ph) - t0
                    actf(PF[:, g, 15 + t0:15 + t0 + np_], PS[b][:, 0:np_], AF.Copy)
                    if t1 > ph:
                        actf(PSm[:, g, :, 15:19], _r4(PS[b][:, np_:np_ + sh]), AF.Copy)
                cp(PF[:, g, 0:15], pref[:, g * 15:(g + 1) * 15])
            for f in range(8):
                sx, sbb, scc = ring_alloc(), ring_alloc(), ring_alloc()
                ring_load(sx, [(0, 16, wcols(w_in[l], OFF_CONV + f * 128))])
                ring_load(sbb, [(0, 16, wcols(w_in[l], OFF_CONV + 1024 + f * 128))])
                ring_load(scc, [(0, 16, wcols(w_in[l], OFF_CONV + 2048 + f * 128))])
                cp(ZF[:, 0:2], pref[:, 60 + 2 * f:62 + 2 * f])
                if sh:
                    cp(ZS[:, :, 0:2], ZSP[:, f, :, :])
                for bi, (t0, t1) in enumerate(blks):
                    n = t1 - t0
                    np_ = min(t1, ph) - t0
                    bx, bb_, bcx = P.bank(), P.bank(), P.bank()
                    proj(sx, 0, 16, HH, t0, t1, PS[bx][:, 0:n])
                    proj(sbb, 0, 16, HH, t0, t1, PS[bb_][:, 0:n])
                    proj(scc, 0, 16, HH, t0, t1, PS[bcx][:, 0:n])
                    actf(XC[:, 0:n], PS[bx][:, 0:n], AF.Copy)
                    actf(BC[:, t0:t1], PS[bb_][:, 0:n], AF.Copy)
                    tt(ZF[:, 2 + t0:2 + t0 + np_], PS[bcx][:, 0:np_], XC[:, 0:np_], ALU.mult)
                    if t1 > ph:
                        tt(ZS[:, :, 2:6], _r4(PS[bcx][:, np_:np_ + sh]), _r4(XC[:, np_:np_ + sh]), ALU.mult)
                cv = CV[:, 0:ph]
                ts(cv, ZF[:, 0:ph], WCV[:, l, 0, f:f + 1], None, ALU.mult)
                stt(cv, ZF[:, 1:1 + ph], WCV[:, l, 1, f:f + 1], cv, ALU.mult, ALU.add)
                stt(cv, ZF[:, 2:2 + ph], WCV[:, l, 2, f:f + 1], cv, ALU.mult, ALU.add)
                tt(BR[:, 4 + f, 0:ph], cv, BC[:, 0:ph], ALU.mult)
                if sh:
                    cvs = _r4(CV[:, ph:th])
                    ts(cvs, ZS[:, :, 0:4], WCV[:, l, 0, f:f + 1], None, ALU.mult)
                    stt(cvs, ZS[:, :, 1:5], WCV[:, l, 1, f:f + 1], cvs, ALU.mult, ALU.add)
                    stt(cvs, ZS[:, :, 2:6], WCV[:, l, 2, f:f + 1], cvs, ALU.mult, ALU.add)
                    tt(BR[:, 4 + f, ph:th], CV[:, ph:th], BC[:, ph:th], ALU.mult)
                    cp(OCS[:, f, :, :], ZS[:, :, 4:6])
                if hf is None:
                    ts(tail_dst[:, 60 + 2 * f:62 + 2 * f], ZF[:, ph:ph + 2], PAR[:, 0:1], None, ALU.mult)
                else:
                    cp(tail_dst[:, 60 + 2 * f:62 + 2 * f], ZF[:, ph:ph + 2])
            if sh:
                store("o_ocs", o_conv_s[l, hf], OCS.rearrange("p f b r -> p (f b r)"))
            for g in range(4):
                if g % 2 == 0:
                    s = ring_alloc(2)
                    pv = load_panel(s, 2, wcols(w_in[l], OFF_U + g * 128, 2))
                for bi, (t0, t1) in enumerate(blks):
                    b = P.bank()
                    projp(pv, g % 2, 0, 16, HH, t0, t1, PS[b][:, 0:t1 - t0])
                    gelu_from_psum(PS[b][:, 0:t1 - t0], BR[:, 12 + g, t0:t1], 128, t1 - t0)
            s4 = ring_alloc(4)
            pv4 = load_panel(s4, 4, wcols(w_in[l], OFF_V, 4))
            chunks = [(c * 128, 128) for c in range(ph // 128)] + ([(ph, sh)] if sh else [])
            for ci, (c0, cn) in enumerate(chunks):
                b = P.bank()
                q = ci % 2
                is_s = (c0 == ph)
                mm_group(PS[b][0:cn, :], [(HH[:, k, c0:c0 + cn], pv4[:, k, :]) for k in range(16)],
                         [pv4, HH[:, :, c0:c0 + cn]])
                vt = VT[0:cn, :]
                gelu_from_psum(PS[b][0:cn, :], vt, cn, 512)
                actf(GU[0:cn, 1, :], vt, AF.Square)
                vec(lambda e, cn=cn, q=q: e.reduce_sum(out=SSQ[0:cn, q:q + 1], in_=GU[0:cn, 1, :], axis=AX.X),
                    [GU[0:cn, 1, :]], [SSQ[0:cn, q:q + 1]])
                rstd_from(SSQ[0:cn, q:q + 1], SSQ[0:cn, 2 + q:3 + q], 512)
                stt(vt, vt, SSQ[0:cn, 2 + q:3 + q], SGN[0:cn, :], ALU.mult, ALU.mult)
                actf(VB[0:cn, :], vt, AF.Copy)
                if hf == 1 and (not is_s) and c0 == ph - 128:
                    store("o_vt", o_v_p[l], vt)
                if is_s:
                    store("o_vt", o_v_s[l, hf * SH:(hf + 1) * SH, :], vt)
                b2 = P.bank()
                if not is_s:
                    def fn(e, b2=b2):
                        e.matmul(PS[b2][:, :], lhsT=ONESF[0:1, :], rhs=BSG[0:1, :], start=True, stop=False)
                        ins = None
                        for g in range(4):
                            ins = e.matmul(PS[b2][:, g * 128:(g + 1) * 128], lhsT=VB[:, g * 128:(g + 1) * 128],
                                           rhs=WT[:, g, :], start=False, stop=(g == 3))
                        return ins
                    P.op("pe", fn, reads=[VB, WT, BSG, ONESF], writes=[PS[b2][:, :]])
                    dst = BR[:, 12:16, c0:c0 + 128]
                    tt(dst, dst, PS[b2][:, :].rearrange("p (g i) -> p g i", i=128), ALU.mult)
                else:
                    def fn(e, b2=b2):
                        ins = None
                        for g in range(4):
                            o_ = PS[b2][:, g * SH:(g + 1) * SH]
                            e.matmul(_r4(o_), lhsT=ONESF[0:1, :],
                                     rhs=BSG[0:1, g * 128:g * 128 + 4].unsqueeze(1).to_broadcast([1, NSH, 4]),
                                     start=True, stop=False)
                            ins = e.matmul(o_, lhsT=VB[0:SH, g * 128:(g + 1) * 128], rhs=BD[0:SH, g, :],
                                           start=False, stop=True)
                        return ins
                    P.op("pe", fn, reads=[VB[0:SH, :], BD, BSG, ONESF], writes=[PS[b2][:, 0:4 * SH]])
                    dst = BR[:, 12:16, ph:th]
                    tt(dst, dst, PS[b2][:, 0:4 * SH].rearrange("p (g i) -> p g i", i=SH), ALU.mult)
            for g in range(4):
                w = POOL_W[g]
                NP = 15 + ph
                cur, curs = PF[:, g, :], PSm[:, g, :, :]
                k, step = 1, 0
                while k < w:
                    dst, dsts = (TA, TB)[step % 2], (SA, SB_)[step % 2]
                    tt(dst[:, k:NP], cur[:, k:NP], cur[:, 0:NP - k], ALU.add)
                    if sh:
                        tt(dsts[:, :, k:19], curs[:, :, k:19], curs[:, :, 0:19 - k], ALU.add)
                    cur, curs = dst, dsts
                    k *= 2
                    step += 1
                stt(DD[:, 0:ph], cur[:, 15:NP], 1.0 / w, PF[:, g, 15:NP], ALU.mult, ALU.subtract)
                if hf == 0:
                    tmp = T15[:, 0:15]
                    tt(tmp, cur[:, 15:30], INVC[:, g, :], ALU.mult)
                    tt(DD[:, 0:15], tmp, PF[:, g, 15:30], ALU.subtract)
                if hf is None:
                    ts(tail_dst[:, g * 15:(g + 1) * 15], PF[:, g, ph:ph + 15], PAR[:, 0:1], None, ALU.mult)
                else:
                    cp(tail_dst[:, g * 15:(g + 1) * 15], PF[:, g, ph:ph + 15])
                if sh:
                    stt(_r4(DD[:, ph:th]), curs[:, :, 15:19], 1.0 / w, PSm[:, g, :, 15:19], ALU.mult, ALU.subtract)
                for bi, (t0, t1) in enumerate(blks):
                    b = P.bank()
                    mm_group(PS[b][:, 0:t1 - t0], [(WPG[:, g, :], DD[:, t0:t1])], [WPG[:, g, :], DD[:, t0:t1]])
                    actf(BR[:, g, t0:t1], PS[b][:, 0:t1 - t0], AF.Identity, scale=PSC[:, l, g:g + 1])
            if sh:
                store("o_psm", o_pool_s[l, hf], PSm[:, :, :, 4:19])
            if hf == 1:
                store("o_tail", o_pool_p[l], tail_dst[:, 0:60])
                store("o_tail", o_conv_p[l], tail_dst[:, 60:76], newgen=False)

            def merged_tile(f):
                sa = ring_alloc(2)
                pa = load_panel(sa, 2, wcols(w_mg[l], f * 4 * 128, 2))
                sb_ = ring_alloc(2)
                pb = load_panel(sb_, 2, wcols(w_mg[l], (f * 4 + 2) * 128, 2))
                mq = f % 2
                for bi, (t0, t1) in enumerate(blks):
                    n = t1 - t0
                    for bq in range(2):
                        b = P.bank()
                        projp(pa, bq, 0, 16, HH, t0, t1, PS[b][:, 0:n])
                        actf(GT[:, bi, bq, 0:n], PS[b][:, 0:n], AF.Sigmoid)
                for bi, (t0, t1) in enumerate(blks):
                    n = t1 - t0
                    b = P.bank()
                    projp(pb, 0, 0, 16, HH, t0, t1, PS[b][:, 0:n])
                    actf(GT[:, bi, 2, 0:n], PS[b][:, 0:n], AF.Sigmoid)
                    bb = []
                    for bq, (k0, k1) in enumerate(((0, 4), (4, 12), (12, 16))):
                        b = P.bank()
                        bb.append(b)
                        projp(pb, 1, k0, k1, BR, t0, t1, PS[b][:, 0:n])
                    t0_, t1_ = TT[:, bi, 0, 0:n], TT[:, bi, 1, 0:n]
                    tt(t0_, PS[bb[0]][:, 0:n], GT[:, bi, 0, 0:n], ALU.mult)
                    tt(t1_, PS[bb[1]][:, 0:n], GT[:, bi, 1, 0:n], ALU.mult)
                    tt(t0_, t0_, t1_, ALU.add)
                    tt(t1_, PS[bb[2]][:, 0:n], GT[:, bi, 2, 0:n], ALU.mult)
                    tt(MG[:, mq, t0:t1], t0_, t1_, ALU.add)

            def wout_pair(f0):
                so = []
                for f in (f0, f0 + 1):
                    s_ = ring_alloc()
                    ring_load(s_, [(0, 16, w_out[l][f * 128:(f + 1) * 128, :].rearrange("p (o c) -> p o c", c=128))])
                    so.append(s_)
                for o in range(NFT):
                    for bi, (t0, t1) in enumerate(blks):
                        b = P.bank()
                        mm_group(PS[b][:, 0:t1 - t0],
                                 [(RING[:, so[0], o, :], MG[:, 0, t0:t1]), (RING[:, so[1], o, :], MG[:, 1, t0:t1])],
                                 [RING[:, so[0], o, :], RING[:, so[1], o, :], MG[:, :, t0:t1]])
                        resid_update(2, o, XS_, segfn, xo + t0, xo + t1, b)

            for f0 in range(0, NFT, 2):
                merged_tile(f0)
                merged_tile(f0 + 1)
                wout_pair(f0)

        def channel_mlp(l, XS_, segfn, xo, tlen):
            blks = FBLK if tlen == T else [(0, tlen)]
            assert tlen == T or tlen <= 384
            norm_mod(l, 1, HF, XS_, segfn, xo, tlen, blks)
            rr = [0]
            for sl in range(8):
                for jt in range(8):
                    if jt % 2 == 0:
                        s = ring_alloc(2)
                        pv = load_panel(s, 2, wcols(w_ff1[l], sl * 1024 + jt * 128, 2))
                    for bi, (t0, t1) in enumerate(blks):
                        b = P.bank()
                        n = t1 - t0
                        projp(pv, jt % 2, 0, 16, HF, t0, t1, PS[b][:, 0:n])
                        rq = rr[0] % 2
                        rr[0] += 1
                        actf(RT[:, rq, 0:n], PS[b][:, 0:n], AF.Relu)
                        tt(AF_[:, jt, t0:t1], RT[:, rq, 0:n], RT[:, rq, 0:n], ALU.mult)
                w2 = w_ff2[l][sl * 1024:(sl + 1) * 1024, :]
                for o4 in range(0, NFT, 4):
                    s = ring_alloc(2)
                    p8 = load_panel(s, 2, wcols(w2, o4 * 128, 4), kk=8)
                    for oo in range(4):
                        o = o4 + oo
                        for bi, (t0, t1) in enumerate(blks):
                            b = P.bank()
                            pairs = [(p8[:, k, oo * 128:(oo + 1) * 128], AF_[:, k, t0:t1]) for k in range(8)]
                            mm_group(PS[b][:, 0:t1 - t0], pairs, [p8[:, :, oo * 128:(oo + 1) * 128], AF_[:, :, t0:t1]])
                            resid_update(5, o, XS_, segfn, xo + t0, xo + t1, b)

        for l in range(NL):
            layer_params(l)
            mods(l)
            tail_pass(l)
            if l < 3:
                nh = 128 * (3 - l)
                token_mix(l, XH, halo_segs, 384 - nh, nh, 0, HTAIL, HALO, None)
                channel_mlp(l, XH, halo_segs, 384 - nh, nh)
            token_mix(l, X, main_segs, 0, PH, SH, HALO, CARRY, 0)
            token_mix(l, X, main_segs, TH, PH, SH, CARRY, TAILB, 1)
            channel_mlp(l, X, main_segs, 0, T)

        banks = [P.bank() for _ in FBLK]
        for ft in range(NFT):
            q = ft % 2
            actf(SQ[:, q, :], X[:, ft, :], AF.Square)
            for bi, (t0, t1) in enumerate(FBLK):
                b = banks[bi]
                P.op("pe", lambda e, ft=ft, q=q, t0=t0, t1=t1, b=b: e.matmul(
                    PS[b][:, 0:t1 - t0], lhsT=ONESB, rhs=SQ[:, q, t0:t1], start=(ft == 0), stop=(ft == NFT - 1)),
                    reads=[SQ[:, q, t0:t1], ONESB], writes=[PS[b][:, 0:t1 - t0]])
        for bi, (t0, t1) in enumerate(FBLK):
            rstd_from(PS[banks[bi]][:, 0:t1 - t0], RSTD[:, t0:t1], D)
        cnt = 0
        for ft in range(NFT):
            for bi, (t0, t1) in enumerate(FBLK):
                q = cnt % 2
                cnt += 1
                y = NTB[:, q, 0:t1 - t0]
                stt(y, X[:, ft, t0:t1], NRM[:, 2 * L, ft:ft + 1], RSTD[:, t0:t1], ALU.mult, ALU.mult)
                P.dma("sp", yT[ft * 128:(ft + 1) * 128, t0:t1], y, reads=[y], semkey="outy%d" % q, gen=cnt)
        fin = P.op("sp", lambda e: None, reads=[], writes=[])
        lastk = {}
        for o in P.ops:
            if o.isdma and o.semkey.startswith("o"):
                lastk[o.semkey] = o
        fin.deps = list(lastk.values())


        with nc.Block() as block:
            P.emit(block, sems, dma_sems)
    return nc


_NC_CACHE = {}


def _host_inputs(inp):
    f32 = np.float32
    xp, xs = inp["x_prompt"], inp["x_sample"]
    shared = {k: np.ascontiguousarray(inp[k], dtype=f32) for k in
              ("w_ada", "w_out", "w_ff1", "w_ff2", "w_pool_grp")}
    w_in_ = np.asarray(inp["w_in"], f32)
    shared["w_in4"] = np.ascontiguousarray(w_in_[:, :, :OFF_GATE])
    gates = w_in_[:, :, OFF_GATE:].reshape(L, D, 3, 16, 128).transpose(0, 1, 3, 2, 4)
    wbr = np.concatenate([inp["w_br_pool"], inp["w_br_conv"], inp["w_br_sgu"]], 1).astype(f32, copy=False)
    wbr = wbr.reshape(L, D, 16, 1, 128)
    shared["w_mg"] = np.ascontiguousarray(np.concatenate([gates, wbr], 3).reshape(L, D, 16 * 512))
    nrm = np.concatenate([np.stack([inp["norm1"], inp["norm2"]], 1).reshape(2 * L, D),
                          inp["final_norm"][None]], 0)
    shared["normT"] = np.ascontiguousarray(nrm.reshape(2 * L + 1, 16, 128).transpose(2, 0, 1), f32)
    shared["b_adaT"] = np.ascontiguousarray(inp["b_ada"].reshape(L, 96, 128).transpose(2, 0, 1), f32)
    shared["pscT"] = np.ascontiguousarray(inp["pool_scale"].reshape(L, 4, 128).transpose(2, 0, 1), f32)
    shared["wcvT"] = np.ascontiguousarray(inp["w_conv"].reshape(L, 3, 8, 128).transpose(3, 0, 1, 2), f32)
    shared["sgn"] = np.ascontiguousarray(inp["sgu_norm"], f32)
    shared["wsT"] = np.ascontiguousarray(inp["w_sgu"].transpose(0, 1, 3, 2), f32)
    w4 = inp["w_sgu"][:, :, :4, :4].transpose(0, 1, 3, 2)
    shared["w4rep"] = np.ascontiguousarray(np.tile(w4, (1, 1, NSH, NSH)), f32)
    shared["bsg"] = np.ascontiguousarray(inp["b_sgu"].reshape(L, 512), f32)
    jj, ii = np.meshgrid(np.arange(128), np.arange(128), indexing="ij")
    shared["tri"] = (jj <= ii).astype(f32)
    r, c = np.meshgrid(np.arange(32), np.arange(32), indexing="ij")
    shared["bdm"] = ((r // 4 == c // 4) & (r % 4 <= c % 4)).astype(f32)
    maps = []
    for core in range(NCORE):
        b, half = core // 2, core % 2
        m = dict(shared)
        xpc = xp[b, half * TP:(half + 1) * TP]
        xsc = xs[core * NSB:(core + 1) * NSB].reshape(TS, D)
        xcore = np.concatenate([xpc[:PH], xsc[:SH], xpc[PH:], xsc[SH:]], 0)
        m["xT"] = np.ascontiguousarray(xcore.T, f32)
        cc = np.concatenate([inp["c_prompt"][b:b + 1], inp["c_sample"][core * NSB:(core + 1) * NSB]], 0)
        m["cT"] = np.ascontiguousarray(cc.T, f32)
        sp_ = inp["state_pool"][:, core * NSB:(core + 1) * NSB]
        m["stp"] = np.ascontiguousarray(sp_.reshape(L, 2, NSH, 15, 4, 128).transpose(0, 1, 5, 4, 2, 3), f32)
        sc_ = inp["state_conv"][:, core * NSB:(core + 1) * NSB]
        m["stc"] = np.ascontiguousarray(sc_.reshape(L, 2, NSH, 2, 8, 128).transpose(0, 1, 5, 4, 2, 3), f32)
        pos = half * TP + np.arange(15)
        iv = np.stack([1.0 / np.minimum(pos + 1, w) for w in POOL_W], 0)
        m["invc"] = np.ascontiguousarray(np.broadcast_to(iv[None], (128, 4, 15)), f32)
        m["par"] = np.full((128, 1), float(half), f32)
        if half == 1:
            m["xhT"] = np.ascontiguousarray(xp[b, TP - 384:TP].T, f32)
            m["xt15"] = np.ascontiguousarray(xp[b, TP - 384 - 15:TP - 384].T, f32)
        else:
            m["xhT"] = np.zeros((D, 384), f32)
            m["xt15"] = np.zeros((D, 15), f32)
        maps.append(m)
    return maps


def _assemble(res):
    f32 = np.float32
    B, S = 4, 2 * TP
    y_p = np.empty((B, S, D), f32)
    y_s = np.empty((NCORE * NSB, 4, D), f32)
    pool_p = np.empty((L, B, 15, 512), f32)
    conv_p = np.empty((L, B, 2, 1024), f32)
    v_p = np.empty((L, B, 128, 512), f32)
    pool_s = np.empty((L, NCORE * NSB, 15, 512), f32)
    conv_s = np.empty((L, NCORE * NSB, 2, 1024), f32)
    v_s = np.empty((L, NCORE * NSB, 4, 512), f32)
    for core in range(NCORE):
        r = res[core]
        b, half = core // 2, core % 2
        y = np.asarray(r["yT"]).T
        yp = np.concatenate([y[0:PH], y[TH:TH + PH]], 0)
        ys = np.concatenate([y[PH:TH], y[TH + PH:T]], 0)
        y_p[b, half * TP:(half + 1) * TP] = yp
        y_s[core * NSB:(core + 1) * NSB] = ys.reshape(NSB, 4, D)
        sl = slice(core * NSB, (core + 1) * NSB)
        pool_s[:, sl] = np.asarray(r["o_pool_s"]).transpose(0, 1, 4, 5, 3, 2).reshape(L, NSB, 15, 512)
        conv_s[:, sl] = np.asarray(r["o_conv_s"]).reshape(L, 2, 128, 8, NSH, 2).transpose(0, 1, 4, 5, 3, 2).reshape(L, NSB, 2, 1024)
        v_s[:, sl] = np.asarray(r["o_v_s"]).reshape(L, NSB, 4, 512)
        if half == 1:
            pool_p[:, b] = np.asarray(r["o_pool_p"]).reshape(L, 128, 4, 15).transpose(0, 3, 2, 1).reshape(L, 15, 512)
            conv_p[:, b] = np.asarray(r["o_conv_p"]).reshape(L, 128, 8, 2).transpose(0, 3, 2, 1).reshape(L, 2, 1024)
            v_p[:, b] = np.asarray(r["o_v_p"])
    return (y_p, y_s, pool_p, conv_p, v_p, pool_s, conv_s, v_s)


def kernel(**inputs):
    inp = {k: np.asarray(v) for k, v in inputs.items()}
    if "nc" not in _NC_CACHE:
        _NC_CACHE["nc"] = build_nc()
    nc = _NC_CACHE["nc"]
    maps = _host_inputs(inp)
    res = run_bass_kernel_spmd(nc, maps, core_ids=list(range(NCORE)))
    return _assemble(res.results)
```
